# Optimizing a Trainium2 kernel written in Bass

```python
import jax, jax.numpy as jnp
from jax import lax
import numpy as np

D_MODEL = 1024
BATCH = 2
SEQ = 8192
DEPTH = 2
DEC_BATCH = 8
DEC_SEQ = 8192
PAST_LEN = 128

N_MIXERS = 2
N_ATTN_LAYERS = (DEPTH + 1) // 2
N_POOL_LAYERS = DEPTH // 2
N_HEADS = 8
N_KV_HEADS = 2
HEAD_DIM = 128
GROUP = N_HEADS // N_KV_HEADS
Q_DIM = N_HEADS * HEAD_DIM
KV_DIM = N_KV_HEADS * HEAD_DIM
ATTN_IN = 2 * Q_DIM + 2 * KV_DIM
AXIS_DIM = HEAD_DIM // 2
ROPE_THETA = 10000.0
Q_BLOCK = 128
GRID_W = 64
POOL_WINDOWS = (2, 4, 8, 16)
N_POOL_GROUPS = len(POOL_WINDOWS)
POOL_WIDTH = D_MODEL
POOL_GROUP_WIDTH = POOL_WIDTH // N_POOL_GROUPS
RMS_EPS = 1e-6

kernel_name = "hybrid_axial_gqa_multiscale_pool_encoder"


def _rmsnorm(x, g):
    x32 = x.astype(jnp.float32)
    y = x32 * lax.rsqrt(jnp.mean(x32 * x32, axis=-1, keepdims=True) + RMS_EPS)
    return (y * g.astype(jnp.float32)).astype(x.dtype)


def _axial_rope(s):
    rows = s // GRID_W
    row = jnp.broadcast_to(jnp.arange(rows)[:, None], (rows, GRID_W)).reshape(s).astype(jnp.float32)
    col = jnp.broadcast_to(jnp.arange(GRID_W)[None, :], (rows, GRID_W)).reshape(s).astype(jnp.float32)
    inv = ROPE_THETA ** (-jnp.arange(0, AXIS_DIM, 2, dtype=jnp.float32) / AXIS_DIM)
    ang = jnp.concatenate([row[:, None] * inv, col[:, None] * inv], axis=-1)
    return jnp.cos(ang), jnp.sin(ang)


def _apply_rope(x, cos, sin):
    b, s, h, d = x.shape
    xr = x.reshape(b, s, h, d // 2, 2)
    x0, x1 = xr[..., 0], xr[..., 1]
    c = cos[None, :, None, :]
    sn = sin[None, :, None, :]
    return jnp.stack([x0 * c - x1 * sn, x0 * sn + x1 * c], axis=-1).reshape(b, s, h, d)


def _attention_mixer(h, w_in, q_g, k_g, w_out):
    b, s, _ = h.shape
    q, k, v, gate = jnp.split(h @ w_in, [Q_DIM, Q_DIM + KV_DIM, Q_DIM + 2 * KV_DIM], axis=-1)
    q = q.reshape(b, s, N_HEADS, HEAD_DIM)
    k = k.reshape(b, s, N_KV_HEADS, HEAD_DIM)
    v = v.reshape(b, s, N_KV_HEADS, HEAD_DIM)
    cos, sin = _axial_rope(s)
    q = (_apply_rope(_rmsnorm(q, q_g).astype(jnp.float32), cos, sin) * (HEAD_DIM ** -0.5)).astype(h.dtype)
    k = _apply_rope(_rmsnorm(k, k_g).astype(jnp.float32), cos, sin).astype(h.dtype)
    nb = s // Q_BLOCK
    qb = q.reshape(b, nb, Q_BLOCK, N_KV_HEADS, GROUP, HEAD_DIM).transpose(1, 0, 2, 3, 4, 5)

    def block(qblk):
        sc = jnp.einsum('bqkgd,bskd->bkgqs', qblk, k, preferred_element_type=jnp.float32)
        p = jax.nn.softmax(sc, axis=-1)
        return jnp.einsum('bkgqs,bskd->bqkgd', p.astype(v.dtype), v)

    o = lax.map(block, qb)
    o = o.transpose(1, 0, 2, 3, 4, 5).reshape(b, s, Q_DIM)
    return (o * jax.nn.silu(gate)) @ w_out


def _pool_mixer(h, w_in, w_group, scale, w_out):
    b, s, _ = h.shape
    u, gate = jnp.split(h @ w_in, 2, axis=-1)
    u32 = u.astype(jnp.float32)
    cs = jnp.concatenate([jnp.zeros((b, 1, POOL_WIDTH), jnp.float32), jnp.cumsum(u32, axis=1)], axis=1)
    t = jnp.arange(s)
    pooled = []
    for j, w in enumerate(POOL_WINDOWS):
        lo = jnp.clip(t - w // 2, 0, s)
        hi = jnp.clip(t - w // 2 + w, 0, s)
        csg = cs[..., j * POOL_GROUP_WIDTH:(j + 1) * POOL_GROUP_WIDTH]
        cnt = (hi - lo).astype(jnp.float32)[None, :, None]
        pooled.append((jnp.take(csg, hi, axis=1) - jnp.take(csg, lo, axis=1)) / cnt)
    mix = (jnp.concatenate(pooled, axis=-1) - u32).astype(h.dtype)
    mix = jnp.einsum('bsgc,gcd->bsgd', mix.reshape(b, s, N_POOL_GROUPS, POOL_GROUP_WIDTH), w_group)
    mix = mix.reshape(b, s, POOL_WIDTH) * scale
    return (mix * jax.nn.silu(gate)) @ w_out


def _trunk(x, c, norm_g, ada_w, ada_b, attn_w_in, attn_q_norm, attn_k_norm, attn_w_out,
           pool_w_in, pool_w_group, pool_scale, pool_w_out):
    cs = jax.nn.silu(c)
    for i in range(DEPTH):
        mod = cs @ ada_w[i] + ada_b[i]
        shift, scl, gate = jnp.split(mod, 3, axis=-1)
        h = _rmsnorm(x, norm_g[i]) * (1 + scl[:, None, :]) + shift[:, None, :]
        j = i // N_MIXERS
        if i % N_MIXERS == 0:
            out = _attention_mixer(h, attn_w_in[j], attn_q_norm[j], attn_k_norm[j], attn_w_out[j])
        else:
            out = _pool_mixer(h, pool_w_in[j], pool_w_group[j], pool_scale[j], pool_w_out[j])
        x = x + gate[:, None, :] * out
    return x


def setup_inputs(seed: int = 0) -> dict:
    key = jax.random.key(seed)
    ks = jax.random.split(key, 16)
    f32 = jnp.float32
    nrm = lambda k, shp: jax.random.normal(k, shp, f32)
    return {
        "x_prompt": nrm(ks[0], (BATCH, SEQ, D_MODEL)),
        "x_sample": nrm(ks[1], (DEC_BATCH, DEC_SEQ, D_MODEL)),
        "c_prompt": nrm(ks[2], (BATCH, D_MODEL)),
        "c_sample": nrm(ks[3], (DEC_BATCH, D_MODEL)),
        "norm_g": 1.0 + 0.02 * nrm(ks[4], (DEPTH, D_MODEL)),
        "ada_w": nrm(ks[5], (DEPTH, D_MODEL, 3 * D_MODEL)) * (0.5 * D_MODEL ** -0.5),
        "ada_b": 0.01 * nrm(ks[6], (DEPTH, 3 * D_MODEL)),
        "attn_w_in": nrm(ks[7], (N_ATTN_LAYERS, D_MODEL, ATTN_IN)) * D_MODEL ** -0.5,
        "attn_q_norm": 1.0 + 0.02 * nrm(ks[8], (N_ATTN_LAYERS, HEAD_DIM)),
        "attn_k_norm": 1.0 + 0.02 * nrm(ks[9], (N_ATTN_LAYERS, HEAD_DIM)),
        "attn_w_out": nrm(ks[10], (N_ATTN_LAYERS, Q_DIM, D_MODEL)) * Q_DIM ** -0.5,
        "pool_w_in": nrm(ks[11], (N_POOL_LAYERS, D_MODEL, 2 * POOL_WIDTH)) * D_MODEL ** -0.5,
        "pool_w_group": nrm(ks[12], (N_POOL_LAYERS, N_POOL_GROUPS, POOL_GROUP_WIDTH, POOL_GROUP_WIDTH)) * POOL_GROUP_WIDTH ** -0.5,
        "pool_scale": 1.0 + 0.1 * nrm(ks[13], (N_POOL_LAYERS, POOL_WIDTH)),
        "pool_w_out": nrm(ks[14], (N_POOL_LAYERS, POOL_WIDTH, D_MODEL)) * POOL_WIDTH ** -0.5,
    }


def reference(x_prompt, x_sample, c_prompt, c_sample, norm_g, ada_w, ada_b, attn_w_in,
              attn_q_norm, attn_k_norm, attn_w_out, pool_w_in, pool_w_group, pool_scale, pool_w_out):
    y_prompt = _trunk(x_prompt, c_prompt, norm_g, ada_w, ada_b, attn_w_in, attn_q_norm, attn_k_norm,
                      attn_w_out, pool_w_in, pool_w_group, pool_scale, pool_w_out)
    y_sample = _trunk(x_sample, c_sample, norm_g, ada_w, ada_b, attn_w_in, attn_q_norm, attn_k_norm,
                      attn_w_out, pool_w_in, pool_w_group, pool_scale, pool_w_out)
    return (y_prompt, y_sample)
```

```python
import numpy as np
from contextlib import ExitStack
import concourse.bass as bass
import concourse.mybir as mybir
from concourse.bass_utils import run_bass_kernel_spmd

F32 = mybir.dt.float32
BF16 = mybir.dt.bfloat16
AF = mybir.ActivationFunctionType
ALU = mybir.AluOpType
AX = mybir.AxisListType

D = 1024
NCH = 8
HD = 128
NH = 8
NKV = 2
GRID_W = 64
WINS = (2, 4, 8, 16)
EPS = 1e-6
ENGS = ("pe", "act", "dve", "pool", "sp")


class _Op:
    __slots__ = ("eng", "build", "deps", "idx", "sig", "chan", "val", "is_dma")

    def __init__(self, eng, build, deps, is_dma, chan):
        self.eng, self.build, self.deps, self.is_dma, self.chan = eng, build, deps, is_dma, chan
        self.sig = False
        self.val = None
        self.idx = None


class Prog:
    def __init__(self):
        self.ops = {e: [] for e in ENGS}
        self.lastw = {}
        self.readers = {}
        self.lastacc = {}
        self.final = []
        self.chan_count = {}
        self.fence_deps = []
        self.last_dma = {}

    @staticmethod
    def _stream(op):
        return ("dma", op.chan) if op.is_dma else op.eng

    def op(self, eng, build, reads=(), writes=(), excl=(), chan=None, extra=()):
        is_dma = chan is not None
        deps = {}

        def add(d):
            if d is None:
                return
            if (not is_dma) and (not d.is_dma) and d.eng == "pe" and eng == "pe":
                return
            s = self._stream(d)
            cur = deps.get(s)
            if cur is None or d.idx > cur.idx:
                deps[s] = d

        for k in list(reads) + list(writes):
            add(self.lastw.get(k))
        for k in writes:
            for r in self.readers.get(k, {}).values():
                add(r)
        for k in excl:
            for e2, o2 in self.lastacc.get(k, {}).items():
                if e2 != eng:
                    add(o2)
        for d in extra:
            add(d)
        for d in self.fence_deps:
            add(d)
        o = _Op(eng, build, list(deps.values()), is_dma, chan)
        if is_dma:
            c = self.chan_count.get(chan, 0) + 1
            self.chan_count[chan] = c
            o.idx = c
            self.last_dma[chan] = o
        else:
            o.idx = len(self.ops[eng])
        self.ops[eng].append(o)
        for k in writes:
            self.lastw[k] = o
            self.readers[k] = {}
        for k in reads:
            self.readers.setdefault(k, {})[self._stream(o)] = o
        for k in excl:
            self.lastacc.setdefault(k, {})[eng] = o
        return o

    def dma(self, build, reads=(), writes=(), chan=None, final=False, extra=()):
        o = self.op("sp", build, reads, writes, chan=chan, extra=extra)
        if final:
            self.final.append(o)
        return o

    def fence(self):
        deps = []
        for e in ENGS:
            for o in reversed(self.ops[e]):
                if not o.is_dma:
                    deps.append(o)
                    break
        deps.extend(self.last_dma.values())
        self.fence_deps = deps

    def channels(self):
        return sorted(self.chan_count.keys())

    def emit(self, block, sems, dma_sems):
        for e in ENGS:
            for o in self.ops[e]:
                for d in o.deps:
                    d.sig = True
        for o in self.final:
            o.sig = True
        for e in ENGS:
            cnt = 0
            for o in self.ops[e]:
                if o.is_dma:
                    o.val = 16 * o.idx
                    o.sig = True
                elif o.sig:
                    cnt += 1
                    o.val = cnt

        def semof(d):
            return dma_sems[d.chan] if d.is_dma else sems[d.eng]

        def run(e, engobj):
            seen = {}
            for o in self.ops[e]:
                for d in o.deps:
                    s = self._stream(d)
                    if seen.get(s, 0) >= d.val:
                        continue
                    seen[s] = d.val
                    engobj.wait_ge(semof(d), d.val)
                ins = o.build(engobj)
                if o.sig:
                    if o.is_dma:
                        ins.then_inc(dma_sems[o.chan], 16)
                    else:
                        ins.then_inc(sems[e], 1)
            if e == "sp":
                for o in self.final:
                    s = self._stream(o)
                    if seen.get(s, 0) >= o.val:
                        continue
                    seen[s] = o.val
                    engobj.wait_ge(semof(o), o.val)

        @block.tensor
        def _(eng):
            run("pe", eng)

        @block.scalar
        def _(eng):
            run("act", eng)

        @block.vector
        def _(eng):
            run("dve", eng)

        @block.gpsimd
        def _(eng):
            run("pool", eng)

        @block.sync
        def _(eng):
            run("sp", eng)


def build_program(S, dbg=False):
    NT = S // 128
    QW = S // 4
    NB1 = QW // 128
    NPAIR = NT // 2
    assert NT % 2 == 0 and NB1 >= 1

    nc = bass.Bass("TRN2", target_bir_lowering=False, dynamic_dma_scratch_size=256)
    P = Prog()

    def din(name, shape, dt=F32):
        return nc.dram_tensor(name, list(shape), dt, kind="ExternalInput").ap()

    xs_d = [din("xs0", [S, D]), din("xs1", [S, D])]
    xq1_d = din("xq1", [QW, D])
    xh_d = din("xh", [128, D])
    cT_d = din("cT", [128, 16])
    rope_d = din("rope", [S, 128])
    ropeq1_d = din("ropeq1", [QW, 128])
    ropeh_d = din("ropeh", [128, 128])
    gcol_d = din("gcol", [128, 16])
    abcol_d = din("abcol", [128, 64])
    adab_d = din("ada_b", [2, 3 * D])
    adaw_d = din("ada_w", [2, D, 3 * D])
    win0_d = din("w_in0", [D, 2560])
    wout0_d = din("w_out0", [D, D])
    win1_d = din("w_in1", [D, 2 * D])
    wgrp_d = din("w_grp", [D, 256])
    wout1_d = din("w_out1", [D, D])
    qg_d = din("qg", [1, 128])
    kg_d = din("kg", [1, 128])
    pscale_d = din("pscale", [1, D])
    corr_d = din("corr", [1, 4 * 4 * 128])
    maskh_d = din("maskh", [1, 16])
    y_d = [nc.dram_tensor("y0", [S, D], F32, kind="ExternalOutput").ap(),
           nc.dram_tensor("y1", [QW, D], F32, kind="ExternalOutput").ap()]

    es = ExitStack()
    with es:
        def sb(name, shape, dt):
            return es.enter_context(nc.sbuf_tensor("s_" + name, shape, dt))

        PS = [es.enter_context(nc.psum_tensor("p_%d" % i, [128, 1024], F32)) for i in range(4)]

        def bank(i):
            return PS[i // 2][:, (i % 2) * 512:(i % 2) * 512 + 512]

        def bk(*ids):
            return ["b%d" % i for i in ids]

        TPB = PS[3][:, 512:1024].bitcast(BF16)
        PM = PS[3][:, 512:1024]

        ident = sb("ident", [128, 128], BF16)
        onesel = sb("onesel", [64, 128], BF16)
        mhalf = sb("mhalf", [128, 1], F32)
        negM = sb("negM", [128, 1], F32)
        small = sb("small", [128, 64], F32)
        cTt = sb("cTt", [128, 16], F32)
        scT = sb("scT", [128, 16], F32)
        gcol = sb("gcol", [128, 16], F32)
        abcol = sb("abcol", [128, 64], F32)
        modc = sb("modc", [128, 64], F32)
        Acol = sb("Acol", [128, 16], F32)
        c1col = sb("c1col", [128, 16], F32)
        gtab = sb("gtab", [128, 2, 128], F32)
        maskh = sb("maskh", [128, 16], F32)
        UH = sb("UH", [128, 8, 16], F32)
        c0hl = sb("c0hl", [64, 2560], BF16)
        W0qg = sb("W0qg", [128, NCH, 2048], BF16)
        Wo0 = sb("Wo0", [128, NCH, D], BF16)
        W1 = sb("W1", [128, NCH, 2 * D], BF16)
        Wg = sb("Wg", [128, NCH, 256], BF16)
        Wo1 = sb("Wo1", [128, NCH, D], BF16)
        KT = sb("KT", [128, NKV, S], BF16)
        VA = sb("VA", [128, NT, NKV, 129], BF16)

        WORK_BYTES = 51200
        arena = sb("arena", [128, WORK_BYTES // 4], F32)
        apos = [0]

        def view(ap, dt, shape):
            if dt == BF16:
                ap = ap.bitcast(BF16)
            if len(shape) == 2:
                return ap.rearrange("p (a b) -> p a b", a=shape[0])
            if len(shape) == 3:
                return ap.rearrange("p (a b c) -> p a b c", a=shape[0], b=shape[1])
            return ap

        def carve(nbytes, dt, shape):
            assert nbytes % 4 == 0
            a = apos[0]
            apos[0] += nbytes // 4
            assert apos[0] * 4 <= WORK_BYTES, "arena overflow %d" % (apos[0] * 4)
            return view(arena[:, a:a + nbytes // 4], dt, shape)

        xt = [carve(4096, F32, [D]) for _ in range(3)]
        kvbase = apos[0]
        tokbf = carve(2048, BF16, [D])
        AO = carve(2048, BF16, [D])
        xnT = carve(2048, BF16, [NCH, 128])
        thb = carve(1024, BF16, [512])
        scrA = carve(2560, F32, [640])
        scrB = carve(2048, F32, [512])
        QT = [carve(2048, BF16, [NH, 128]) for _ in range(2)]
        sg = [carve(2048, BF16, [D]) for _ in range(2)]
        PT = [carve(2048, BF16, [D]) for _ in range(2)]
        cs = carve(512, F32, [128])
        tq = carve(1024, F32, [4, 64])
        UT = [carve(8 * 144 * 4, F32, [8, 144]) for _ in range(2)]
        sg1T = [carve(2048, BF16, [NCH, 128]) for _ in range(2)]
        q_words = apos[0]
        apos[0] = 0
        W0kv = carve(8192, BF16, [NCH, 512])
        kv_x = [carve(4096, F32, [D]) for _ in range(4)]
        kv_cs = [carve(512, F32, [128]) for _ in range(3)]
        kv_tq = [carve(1024, F32, [4, 64]) for _ in range(2)]
        kv_xn = [carve(2048, BF16, [D]) for _ in range(2)]
        kv_xnT = [carve(2048, BF16, [NCH, 128]) for _ in range(2)]
        kv_krot = [carve(512, BF16, [256]) for _ in range(2)]
        kv_junk = carve(2048, BF16, [D])
        kv_junkA = carve(512, BF16, [256])
        kv_kn = carve(1024, F32, [256])
        kv_tmp = carve(1024, F32, [256])
        kv_words = apos[0]
        apos[0] = max(q_words, kv_words)

        kt_words = NKV * S // 2
        va_words = (NT * NKV * 129) // 2
        KTf = KT.reshape([128, NKV * S])[:, :].bitcast(F32)
        VAf = VA.reshape([128, NT * NKV * 129])[:, 0:2 * va_words].bitcast(F32)
        if kt_words >= 8192 and va_words >= 7680:
            reg_kt, reg_va = KTf, VAf
        else:
            reg_kt = sb("prep_a", [128, 4096], F32)[:, :]
            reg_va = sb("prep_b", [128, 7680], F32)[:, :]
        adaw_bufs = 2 if reg_kt.shape[1] >= 8192 else 1
        ar_off = 2048
        reg_ar = arena[:, ar_off:WORK_BYTES // 4]
        ut_off = WORK_BYTES // 4 - ar_off
        assert ut_off >= 10240, ut_off

        def mm(out, lhsT, rhs, start, stop, reads, banks, skip=False):
            return P.op("pe", lambda e: e.matmul(out, lhsT=lhsT, rhs=rhs, start=start, stop=stop,
                                                 skip_group_check=skip),
                        reads=reads, writes=bk(*banks), excl=bk(*banks))

        def transposes(src, n, reads):
            for i in range(n):
                P.op("pe", lambda e, i=i: e.transpose(out=TPB[:, i * 128:(i + 1) * 128],
                                                      in_=src[:, i * 128:(i + 1) * 128], identity=ident[:, :]),
                     reads=list(reads) + ["ident"], writes=bk(7), excl=bk(7))

        iot = scrA[:, 0:128]
        P.op("pool", lambda e: e.iota(iot, [[1, 128]], base=0, channel_multiplier=-1,
                                      allow_small_or_imprecise_dtypes=True), writes=["scrA"])
        P.op("dve", lambda e: e.tensor_single_scalar(out=ident[:, :], in_=iot, scalar=0.0, op=ALU.is_equal),
             reads=["scrA"], writes=["ident"])
        P.op("dve", lambda e: e.memset(onesel[:, :], 0.0), writes=["onesel"])
        P.op("dve", lambda e: e.memset(onesel[0:1, :], 1.0), writes=["onesel"])
        P.op("dve", lambda e: e.memset(onesel[32:33, :], 1.0), writes=["onesel"])
        P.op("dve", lambda e: e.memset(mhalf[:, :], -0.5), writes=["mhalf"])
        P.op("dve", lambda e: e.memset(c0hl[:, :], 0.0), writes=["c0hl"])
        P.dma(lambda e: e.dma_start(out=cTt[:, :], in_=cT_d[:, :]), writes=["cTt"], chan="c_cT")
        P.dma(lambda e: e.dma_start(out=gcol[:, :], in_=gcol_d[:, :]), writes=["gcol"], chan="c_gcol")
        P.dma(lambda e: e.dma_start(out=abcol[:, :], in_=abcol_d[:, :]), writes=["abcol"], chan="c_abcol")
        P.dma(lambda e: e.dma_start(out=gtab[:, 0, :], in_=qg_d[0:1, :].partition_broadcast(128)), writes=["gtab"], chan="c_qg")
        P.dma(lambda e: e.dma_start(out=gtab[:, 1, :], in_=kg_d[0:1, :].partition_broadcast(128)), writes=["gtab"], chan="c_kg")
        P.dma(lambda e: e.dma_start(out=maskh[:, :], in_=maskh_d[0:1, :].partition_broadcast(128)), writes=["maskh"], chan="c_maskh")
        P.op("dve", lambda e: e.tensor_reduce(out=small[:, 0:2], in_=gtab[:, :, :], axis=AX.X, op=ALU.max,
                                              apply_absolute_value=True), reads=["gtab"], writes=["small01"])
        P.op("dve", lambda e: e.tensor_tensor(out=small[:, 2:3], in0=small[:, 0:1], in1=small[:, 1:2], op=ALU.mult),
             reads=["small01"], writes=["small2"])
        P.op("dve", lambda e: e.tensor_scalar(out=negM[:, :], in0=small[:, 2:3], scalar1=-float(np.sqrt(128.0)),
                                              scalar2=None, op0=ALU.mult), reads=["small2"], writes=["negM"])
        P.op("dve", lambda e: e.tensor_scalar(out=gtab[:, 0, :], in0=gtab[:, 0, :], scalar1=float(HD ** -0.5),
                                              scalar2=None, op0=ALU.mult), reads=["small01"], writes=["gtab"])
        P.op("act", lambda e: e.activation(out=scT[:, :], in_=cTt[:, :], func=AF.Tanh, scale=0.5),
             reads=["cTt"], writes=["scT"])
        P.op("dve", lambda e: e.scalar_tensor_tensor(out=scT[:, :], in0=scT[:, :], scalar=1.0, in1=cTt[:, :],
                                                     op0=ALU.add, op1=ALU.mult), reads=["scT", "cTt"], writes=["scT"])
        P.op("dve", lambda e: e.tensor_scalar(out=scT[:, :], in0=scT[:, :], scalar1=0.5, scalar2=None, op0=ALU.mult),
             reads=["scT"], writes=["scT"])

        gsc_d = nc.dram_tensor("gsc", [2, 128, D], F32).ap()

        def staging():
            st_adaw = [reg_kt[:, (i % adaw_bufs) * 4096:(i % adaw_bufs + 1) * 4096].rearrange("p (k f) -> p k f", k=8) for i in range(2)]
            if adaw_bufs == 2:
                st_adaw.append(reg_va[:, 0:4096].rearrange("p (k f) -> p k f", k=8))
            st_w = [reg_va[:, i * 2560:(i + 1) * 2560] for i in range(2)]
            if adaw_bufs == 2:
                st_w += [reg_kt[:, i * 2560:(i + 1) * 2560] for i in range(3)]
            c0f = reg_va[0:64, 5120:7680]
            o = 0
            hi32 = reg_ar[0:64, o:o + 2560]
            o += 2560
            scbc = reg_ar[:, o:o + 2048].rearrange("p (s k f) -> p s k f", s=2, k=8)
            o += 2048
            gaterow = [[reg_ar[:, o + (s2 * 2 + l) * 1024: o + (s2 * 2 + l + 1) * 1024] for l in range(2)] for s2 in range(2)]
            o += 4096
            abrow = reg_ar[:, o:o + 512]
            o += 512
            psrow = reg_ar[:, o:o + 1024]
            o += 1024
            assert o <= ut_off
            return st_adaw, st_w, c0f, hi32, scbc, gaterow, abrow, psrow

        def mod_pass():
            P.fence()
            st_adaw, st_w, c0f, hi32, scbc, gaterow, abrow, psrow = staging()
            P.op("dve", lambda e: e.memset(scbc, 1.0), writes=["scbc"])
            for s2 in range(2):
                for kc in range(8):
                    P.op("dve", lambda e, kc=kc, s2=s2: e.tensor_scalar(out=scbc[:, s2, kc, :], in0=scbc[:, s2, kc, :],
                                                                        scalar1=scT[:, kc * 2 + s2:kc * 2 + s2 + 1], scalar2=None,
                                                                        op0=ALU.mult), reads=["scT"], writes=["scbc"])
            nst = 0
            for l in range(2):
                for cb in range(6):
                    nb_ = len(st_adaw) if adaw_bufs == 2 else 1
                    buf = st_adaw[nst % nb_]
                    key = "st_adaw%d" % (nst % nb_)
                    nst += 1
                    P.dma(lambda e, buf=buf, l=l, cb=cb: e.dma_start(
                        out=buf, in_=adaw_d[l, :, cb * 512:(cb + 1) * 512].rearrange("(k p) f -> p k f", p=128)),
                        writes=[key], chan=key)
                    if cb < 4:
                        for j4 in range(4):
                            j = cb * 4 + j4
                            c2 = (l * 16 + j) * 2
                            for kc in range(8):
                                mm(PM[:, c2:c2 + 2], buf[:, kc, j4 * 128:(j4 + 1) * 128], scT[:, kc * 2:kc * 2 + 2],
                                   kc == 0, kc == 7, [key, "scT"], [7])
                    else:
                        half = cb - 4
                        P.dma(lambda e, l=l, half=half: e.dma_start(
                            out=abrow, in_=adab_d[l:l + 1, 2048 + half * 512:2048 + (half + 1) * 512].partition_broadcast(128)),
                            writes=["abrow"], chan="abrow")
                        for s2 in range(2):
                            for kc in range(8):
                                mm(bank(s2), scbc[:, s2, kc, :], buf[:, kc, :], kc == 0, kc == 7, [key, "scbc"], [s2])
                            P.op("dve", lambda e, l=l, half=half, s2=s2: e.tensor_tensor(
                                out=gaterow[s2][l][:, half * 512:(half + 1) * 512], in0=bank(s2), in1=abrow, op=ALU.add),
                                reads=["abrow"] + bk(s2), writes=["gaterow%d_%d" % (s2, l)], excl=bk(s2))
            P.op("dve", lambda e: e.tensor_tensor(out=modc[:, :], in0=PM[:, 0:64], in1=abcol[:, :], op=ALU.add),
                 reads=["abcol"] + bk(7), writes=["modc"], excl=bk(7))
            for l in range(2):
                P.dma(lambda e, l=l: e.dma_start(out=gsc_d[l], in_=gaterow[1][l]), reads=["gaterow1_%d" % l], writes=["gsc"], chan="gsc%d" % l)

        def slot_prep(s):
            P.fence()
            st_adaw, st_w, c0f, hi32, scbc, gaterow_all, abrow, psrow = staging()
            gaterow = gaterow_all[s]
            if s == 1:
                for l in range(2):
                    P.dma(lambda e, l=l: e.dma_start(out=gaterow[l], in_=gsc_d[l]), reads=["gsc"], writes=["gaterow1_%d" % l], chan="gsc%d" % l)
            m3 = modc[:, :].rearrange("p (c s) -> p c s", s=2)
            for l in range(2):
                P.op("dve", lambda e, l=l: e.scalar_tensor_tensor(
                    out=Acol[:, l * 8:(l + 1) * 8], in0=m3[:, l * 16 + 8:l * 16 + 16, s], scalar=1.0,
                    in1=gcol[:, l * 8:(l + 1) * 8], op0=ALU.add, op1=ALU.mult),
                    reads=["modc", "gcol"], writes=["Acol"])

            def shcol(l, kc):
                c = (l * 16 + kc) * 2 + s
                return modc[:, c:c + 1]

            nw = [0]

            def stage_rows(src_ap, ncols):
                i = nw[0] % len(st_w)
                nw[0] += 1
                buf = st_w[i][:, 0:ncols]
                key = "st_w%d" % i
                P.dma(lambda e: e.dma_start(out=buf, in_=src_ap), writes=[key], chan=key)
                return buf, key

            nw = [0]

            def stage_rows(src_ap, ncols):
                i = nw[0] % len(st_w)
                nw[0] += 1
                buf = st_w[i][:, 0:ncols]
                key = "st_w%d" % i
                P.dma(lambda e: e.dma_start(out=buf, in_=src_ap), writes=[key], chan=key)
                return buf, key

            for kc in range(8):
                buf, key = stage_rows(win0_d[kc * 128:(kc + 1) * 128, :], 2560)
                P.op("dve", lambda e, buf=buf, kc=kc: e.tensor_scalar(
                    out=W0qg[:, kc, 0:1024], in0=buf[:, 0:1024], scalar1=Acol[:, kc:kc + 1], scalar2=None, op0=ALU.mult),
                    reads=[key, "Acol"], writes=["W0qg"])
                P.op("dve", lambda e, buf=buf, kc=kc: e.tensor_scalar(
                    out=W0kv[:, kc, :], in0=buf[:, 1024:1536], scalar1=Acol[:, kc:kc + 1], scalar2=None, op0=ALU.mult),
                    reads=[key, "Acol"], writes=["W0kv"])
                P.op("dve", lambda e, buf=buf, kc=kc: e.tensor_scalar(
                    out=W0qg[:, kc, 1024:2048], in0=buf[:, 1536:2560], scalar1=Acol[:, kc:kc + 1], scalar2=None, op0=ALU.mult),
                    reads=[key, "Acol"], writes=["W0qg"])
                for cb in range(5):
                    mm(bank(cb)[0:1, :], shcol(0, kc), buf[:, cb * 512:(cb + 1) * 512], kc == 0, kc == 7,
                       [key, "modc"], [cb])
            for cb in range(5):
                P.op("dve", lambda e, cb=cb: e.tensor_copy(out=c0f[0:1, cb * 512:(cb + 1) * 512], in_=bank(cb)[0:1, :]),
                     reads=bk(cb), writes=["c0f"], excl=bk(cb))
            P.op("dve", lambda e: e.tensor_copy(out=c0hl[0:1, :], in_=c0f[0:1, :]), reads=["c0f"], writes=["c0hl"])
            P.op("dve", lambda e: e.tensor_copy(out=hi32[0:1, :], in_=c0hl[0:1, :]), reads=["c0hl"], writes=["hi32"])
            P.op("dve", lambda e: e.tensor_tensor(out=c0hl[32:33, :], in0=c0f[0:1, :], in1=hi32[0:1, :], op=ALU.subtract),
                 reads=["c0f", "hi32"], writes=["c0hl"])

            for kc in range(8):
                buf, key = stage_rows(wout0_d[kc * 128:(kc + 1) * 128, :], 1024)
                P.op("dve", lambda e, buf=buf, kc=kc: e.scalar_tensor_tensor(
                    out=Wo0[:, kc, :], in0=buf, scalar=0.5, in1=gaterow[0], op0=ALU.mult, op1=ALU.mult),
                    reads=[key, "gaterow%d_0" % s], writes=["Wo0"])

            for kc in range(8):
                buf, key = stage_rows(win1_d[kc * 128:(kc + 1) * 128, :], 2048)
                P.op("dve", lambda e, buf=buf, kc=kc: e.tensor_scalar(
                    out=W1[:, kc, :], in0=buf, scalar1=Acol[:, 8 + kc:8 + kc + 1], scalar2=None, op0=ALU.mult),
                    reads=[key, "Acol"], writes=["W1"])
                for j in range(16):
                    mm(PM[:, 64 + j:64 + j + 1], buf[:, j * 128:(j + 1) * 128], shcol(1, kc),
                       kc == 0 and j == 0, kc == 7 and j == 15, [key, "modc"], [7], skip=True)
            P.op("dve", lambda e: e.tensor_copy(out=c1col[:, :], in_=PM[:, 64:80]), reads=bk(7), writes=["c1col"], excl=bk(7))

            P.dma(lambda e: e.dma_start(out=psrow, in_=pscale_d[0:1, :].partition_broadcast(128)), writes=["psrow"], chan="psrow")
            for kc in range(8):
                buf, key = stage_rows(wgrp_d[kc * 128:(kc + 1) * 128, :], 256)
                g = kc // 2
                P.op("dve", lambda e, buf=buf, kc=kc, g=g: e.tensor_tensor(
                    out=Wg[:, kc, :], in0=buf, in1=psrow[:, g * 256:(g + 1) * 256], op=ALU.mult),
                    reads=[key, "psrow"], writes=["Wg"])
            for kc in range(8):
                buf, key = stage_rows(wout1_d[kc * 128:(kc + 1) * 128, :], 1024)
                P.op("dve", lambda e, buf=buf, kc=kc: e.scalar_tensor_tensor(
                    out=Wo1[:, kc, :], in0=buf, scalar=0.5, in1=gaterow[1], op0=ALU.mult, op1=ALU.mult),
                    reads=[key, "gaterow%d_1" % s], writes=["Wo1"])
            P.fence()

        SM = {"ssx": 8, "vx": 9, "rstdx": 10}

        def rms_scale(src, skey, dst, dkey, junk, jkey, eng_scale, sc0):
            P.op("dve", lambda e: e.scalar_tensor_tensor(out=junk, in0=src, scalar=1.0, in1=src, op0=ALU.mult, op1=ALU.mult,
                                                         accum_out=small[:, sc0:sc0 + 1]),
                 reads=[skey], writes=[jkey, "sm%d" % sc0])
            P.op("pool", lambda e: e.tensor_scalar(out=small[:, sc0 + 1:sc0 + 2], in0=small[:, sc0:sc0 + 1], scalar1=1.0 / D,
                                                   scalar2=EPS, op0=ALU.mult, op1=ALU.add), reads=["sm%d" % sc0], writes=["sm%d" % (sc0 + 1)])
            P.op("pool", lambda e: e.tensor_tensor(out=small[:, sc0 + 2:sc0 + 3], in0=small[:, sc0 + 1:sc0 + 2], in1=mhalf[:, :], op=ALU.pow),
                 reads=["sm%d" % (sc0 + 1), "mhalf"], writes=["sm%d" % (sc0 + 2)])
            if eng_scale == "act":
                P.op("act", lambda e: e.activation(out=dst, in_=src, func=AF.Copy, scale=small[:, sc0 + 2:sc0 + 3]),
                     reads=[skey, "sm%d" % (sc0 + 2)], writes=[dkey])
            else:
                P.op("dve", lambda e: e.tensor_scalar(out=dst, in0=src, scalar1=small[:, sc0 + 2:sc0 + 3], scalar2=None, op0=ALU.mult),
                     reads=[skey, "sm%d" % (sc0 + 2)], writes=[dkey])

        def tr8(src, skey, n, bnk):
            tp = bank(bnk).bitcast(BF16)
            for i in range(n):
                P.op("pe", lambda e, i=i: e.transpose(out=tp[:, i * 128:(i + 1) * 128], in_=src[:, i * 128:(i + 1) * 128],
                                                      identity=ident[:, :]),
                     reads=[skey, "ident"], writes=bk(bnk), excl=bk(bnk))
            return tp

        def rope_tables(src_ap, which, cs_, ckey, tq_, tkey):
            P.dma(lambda e: e.dma_start(out=cs_, in_=src_ap), writes=[ckey], chan=ckey)
            g2 = gtab[:, which, :].rearrange("p (i two) -> p i two", two=2)
            ge, go = g2[:, :, 0], g2[:, :, 1]
            for i, (a, b_) in enumerate([(cs_[:, 0:64], ge), (cs_[:, 64:128], go), (cs_[:, 64:128], ge), (cs_[:, 0:64], go)]):
                P.op("pool", lambda e, i=i, a=a, b_=b_: e.tensor_tensor(out=tq_[:, i, :], in0=a, in1=b_, op=ALU.mult),
                     reads=[ckey, "gtab"], writes=[tkey])

        def norm_rope(raw, rkey, sq, sqkey, nh, sc0, tq_, tkey, dst_bf, dkey):
            n = nh * 128
            for h in range(nh):
                P.op("dve", lambda e, h=h: e.scalar_tensor_tensor(
                    out=sq.bitcast(BF16)[:, 0:128], in0=raw[:, h * 128:(h + 1) * 128], scalar=1.0, in1=raw[:, h * 128:(h + 1) * 128],
                    op0=ALU.mult, op1=ALU.mult, accum_out=small[:, sc0 + h:sc0 + h + 1]), reads=[rkey], writes=[sqkey, "q%d" % sc0])
            P.op("pool", lambda e: e.tensor_scalar(out=small[:, sc0 + 4:sc0 + 4 + nh], in0=small[:, sc0:sc0 + nh], scalar1=1.0 / HD,
                                                   scalar2=EPS, op0=ALU.mult, op1=ALU.add), reads=["q%d" % sc0], writes=["q%d" % (sc0 + 4)])
            P.op("pool", lambda e: e.tensor_tensor(out=small[:, sc0 + 8:sc0 + 8 + nh], in0=small[:, sc0 + 4:sc0 + 4 + nh],
                                                   in1=mhalf[:, 0:1].to_broadcast([128, nh]), op=ALU.pow),
                 reads=["q%d" % (sc0 + 4), "mhalf"], writes=["q%d" % (sc0 + 8)])
            r3 = raw.rearrange("p (h d) -> p h d", h=nh)
            P.op("dve", lambda e: e.tensor_tensor(out=r3, in0=r3, in1=small[:, sc0 + 8:sc0 + 8 + nh].unsqueeze(2).to_broadcast([128, nh, HD]),
                                                  op=ALU.mult), reads=[rkey, "q%d" % (sc0 + 8)], writes=[rkey])
            q4 = raw.rearrange("p (h i two) -> p h i two", h=nh, two=2)
            r4 = dst_bf.rearrange("p (h i two) -> p h i two", h=nh, two=2)
            t1 = sq[:, 0:nh * 64].rearrange("p (h i) -> p h i", h=nh)
            t2 = sq[:, nh * 64:nh * 128].rearrange("p (h i) -> p h i", h=nh)

            def tb(i):
                return tq_[:, i, :].unsqueeze(1).to_broadcast([128, nh, 64])
            P.op("dve", lambda e: e.tensor_tensor(out=t1, in0=q4[:, :, :, 0], in1=tb(0), op=ALU.mult), reads=[rkey, tkey], writes=[sqkey])
            P.op("dve", lambda e: e.tensor_tensor(out=t2, in0=q4[:, :, :, 1], in1=tb(1), op=ALU.mult), reads=[rkey, tkey], writes=[sqkey])
            P.op("dve", lambda e: e.tensor_tensor(out=r4[:, :, :, 0], in0=t1, in1=t2, op=ALU.subtract), reads=[sqkey], writes=[dkey])
            P.op("dve", lambda e: e.tensor_tensor(out=t1, in0=q4[:, :, :, 0], in1=tb(2), op=ALU.mult), reads=[rkey, tkey], writes=[sqkey])
            P.op("dve", lambda e: e.tensor_tensor(out=t2, in0=q4[:, :, :, 1], in1=tb(3), op=ALU.mult), reads=[rkey, tkey], writes=[sqkey])
            P.op("dve", lambda e: e.tensor_tensor(out=r4[:, :, :, 1], in0=t1, in1=t2, op=ALU.add), reads=[sqkey], writes=[dkey])

        def step_all(gens):
            for gs in list(gens):
                if gs[1] > 0:
                    gs[1] -= 1
                    continue
                try:
                    n = next(gs[0])
                    gs[1] = (n or 1) - 1
                except StopIteration:
                    gens.remove(gs)

        def drain(gens):
            while gens:
                step_all(gens)

        def rope_apply(raw, rkey, nh, tq_, tkey, tmp, tmpkey, dst_bf, dkey):
            q4 = raw.rearrange("p (h i two) -> p h i two", h=nh, two=2)
            r4 = dst_bf.rearrange("p (h i two) -> p h i two", h=nh, two=2)
            t1 = tmp[:, 0:nh * 64].rearrange("p (h i) -> p h i", h=nh)
            t2 = tmp[:, nh * 64:nh * 128].rearrange("p (h i) -> p h i", h=nh)

            def tb(i):
                return tq_[:, i, :].unsqueeze(1).to_broadcast([128, nh, 64])
            P.op("dve", lambda e: e.tensor_tensor(out=t1, in0=q4[:, :, :, 0], in1=tb(0), op=ALU.mult), reads=[rkey, tkey], writes=[tmpkey])
            P.op("dve", lambda e: e.tensor_tensor(out=t2, in0=q4[:, :, :, 1], in1=tb(1), op=ALU.mult), reads=[rkey, tkey], writes=[tmpkey])
            P.op("dve", lambda e: e.tensor_tensor(out=r4[:, :, :, 0], in0=t1, in1=t2, op=ALU.subtract), reads=[tmpkey], writes=[dkey])
            P.op("dve", lambda e: e.tensor_tensor(out=t1, in0=q4[:, :, :, 0], in1=tb(2), op=ALU.mult), reads=[rkey, tkey], writes=[tmpkey])
            P.op("dve", lambda e: e.tensor_tensor(out=t2, in0=q4[:, :, :, 1], in1=tb(3), op=ALU.mult), reads=[rkey, tkey], writes=[tmpkey])
            P.op("dve", lambda e: e.tensor_tensor(out=r4[:, :, :, 1], in0=t1, in1=t2, op=ALU.add), reads=[tmpkey], writes=[dkey])

        def kv_tile(s, t):
            x_, xkey = kv_x[t % 4], "kvx%d" % (t % 4)
            cs_, ckey = kv_cs[t % 3], "kvcs%d" % (t % 3)
            tq_, tkey = kv_tq[t % 2], "kvtq%d" % (t % 2)
            xn_, nkey = kv_xn[t % 2], "kvxn%d" % (t % 2)
            xnT_, tkey2 = kv_xnT[t % 2], "kvxnT%d" % (t % 2)
            kr_, rkey = kv_krot[t % 2], "kvkr%d" % (t % 2)
            mb = t % 4
            tb_ = 4 + t % 2
            kb = 6 + t % 2
            c0 = 4 * (t % 4)
            k0 = 8 * (t % 4)
            P.dma(lambda e: e.dma_start(out=x_, in_=xs_d[s][t * 128:(t + 1) * 128, :]), writes=[xkey], chan=xkey)
            yield 2
            P.op("dve", lambda e: e.scalar_tensor_tensor(out=kv_junk, in0=x_, scalar=1.0, in1=x_, op0=ALU.mult, op1=ALU.mult,
                                                         accum_out=small[:, 48 + c0:49 + c0]), reads=[xkey], writes=["kvjunk", "kvs%d" % c0])
            yield 1
            P.op("pool", lambda e: e.tensor_scalar(out=small[:, 49 + c0:50 + c0], in0=small[:, 48 + c0:49 + c0], scalar1=1.0 / D, scalar2=EPS,
                                                   op0=ALU.mult, op1=ALU.add), reads=["kvs%d" % c0], writes=["kvv%d" % c0])
            P.op("pool", lambda e: e.tensor_tensor(out=small[:, 50 + c0:51 + c0], in0=small[:, 49 + c0:50 + c0], in1=mhalf[:, :], op=ALU.pow),
                 reads=["kvv%d" % c0, "mhalf"], writes=["kvr%d" % c0])
            yield 1
            P.op("act", lambda e: e.activation(out=xn_, in_=x_, func=AF.Copy, scale=small[:, 50 + c0:51 + c0]),
                 reads=[xkey, "kvr%d" % c0], writes=[nkey])
            yield 1
            tp = tr8(xn_, nkey, 8, tb_)
            yield 1
            P.op("dve", lambda e: e.tensor_copy(out=xnT_.rearrange("p k t -> p (k t)"), in_=tp), reads=bk(tb_), writes=[tkey2], excl=bk(tb_))
            P.dma(lambda e: e.dma_start(out=cs_, in_=rope_d[t * 128:(t + 1) * 128, :]), writes=[ckey], chan=ckey)
            yield 1
            for kc in range(8):
                mm(bank(mb), xnT_[:, kc, :], W0kv[:, kc, :], kc == 0, False, [tkey2, "W0kv"], [mb])
            mm(bank(mb), onesel[:, :], c0hl[:, 1024:1536], False, True, ["onesel", "c0hl"], [mb])
            yield 1
            P.op("act", lambda e: e.activation(out=VA[:, t, :, 0:128], in_=bank(mb)[:, 256:512].rearrange("p (h d) -> p h d", h=2),
                                               func=AF.Copy), reads=bk(mb), writes=[("VA", t)], excl=bk(mb))
            for h in range(2):
                P.op("act", lambda e, h=h: e.activation(out=kv_junkA[:, h * 128:(h + 1) * 128], in_=bank(mb)[:, h * 128:(h + 1) * 128],
                                                        func=AF.Square, accum_out=small[:, k0 + h:k0 + h + 1]),
                     reads=bk(mb), writes=["kvjunkA", "kvks%d" % k0], excl=bk(mb))
            yield 1
            P.op("pool", lambda e: e.tensor_scalar(out=small[:, k0 + 2:k0 + 4], in0=small[:, k0:k0 + 2], scalar1=1.0 / HD, scalar2=EPS,
                                                   op0=ALU.mult, op1=ALU.add), reads=["kvks%d" % k0], writes=["kvkv%d" % k0])
            P.op("pool", lambda e: e.tensor_tensor(out=small[:, k0 + 4:k0 + 6], in0=small[:, k0 + 2:k0 + 4],
                                                   in1=mhalf[:, 0:1].to_broadcast([128, 2]), op=ALU.pow),
                 reads=["kvkv%d" % k0, "mhalf"], writes=["kvkr%d" % k0])
            g2 = gtab[:, 1, :].rearrange("p (i two) -> p i two", two=2)
            ge, go = g2[:, :, 0], g2[:, :, 1]
            for i, (a, b_) in enumerate([(cs_[:, 0:64], ge), (cs_[:, 64:128], go), (cs_[:, 64:128], ge), (cs_[:, 0:64], go)]):
                P.op("pool", lambda e, i=i, a=a, b_=b_: e.tensor_tensor(out=tq_[:, i, :], in0=a, in1=b_, op=ALU.mult),
                     reads=[ckey, "gtab"], writes=[tkey])
            yield 1
            P.op("dve", lambda e: e.tensor_tensor(out=kv_kn.rearrange("p (h d) -> p h d", h=2),
                                                  in0=bank(mb)[:, 0:256].rearrange("p (h d) -> p h d", h=2),
                                                  in1=small[:, k0 + 4:k0 + 6].unsqueeze(2).to_broadcast([128, 2, HD]), op=ALU.mult),
                 reads=["kvkr%d" % k0] + bk(mb), writes=["kvkn"], excl=bk(mb))
            rope_apply(kv_kn, "kvkn", 2, tq_, tkey, kv_tmp, "kvtmp", kr_, rkey)
            yield 1
            tp2 = tr8(kr_, rkey, 2, kb)
            yield 1
            P.op("dve", lambda e: e.tensor_copy(out=KT[:, :, t * 128:(t + 1) * 128], in_=tp2[:, 0:256].rearrange("p (h t) -> p h t", h=2)),
                 reads=bk(kb), writes=[("KT", t)], excl=bk(kb))

        def kv_phase(s):
            P.op("pool", lambda e: e.memset(VA[:, :, :, 128:129], 1.0), writes=["VAones"])
            gens = []
            t = 0
            while t < NT or gens:
                if t < NT:
                    gens.append([kv_tile(s, t), 0])
                    t += 1
                step_all(gens)

        def o_ap(hh, n, g):
            ob = 4 if g == 0 else 6
            return bank(ob)[:, hh * 129:hh * 129 + n] if hh < 3 else bank(ob + 1)[:, 0:n]

        def o_bank(hh, g):
            ob = 4 if g == 0 else 6
            return ob if hh < 3 else ob + 1

        def front_gen(blk):
            k, par = blk["k"], blk["k"] % 2
            xi = (k - 1) % 3
            xk = "xt%d" % xi
            b6, b7 = side_pair[0]
            tp = None
            rms_scale(xt[xi], xk, tokbf, "tokbf", scrA.bitcast(BF16)[:, 0:D], "scrA", "dve", 8)
            yield 1
            rope_tables(blk["rope"], 0, cs, "cs", tq, "tq")
            yield 5
            b6, b7 = side_pair[0]
            tp = tr8(tokbf, "tokbf", 8, b7)
            yield 1
            P.op("dve", lambda e, b6=b6, b7=b7, tp=tp: e.tensor_copy(out=xnT.rearrange("p k t -> p (k t)"), in_=tp), reads=bk(b7), writes=["xnT"], excl=bk(b7))
            yield 1
            for half in range(2):
                b6, b7 = side_pair[0]
                for kc in range(8):
                    mm(bank(b6), xnT[:, kc, :], W0qg[:, kc, half * 512:(half + 1) * 512], kc == 0, False, ["xnT", "W0qg"], [b6])
                mm(bank(b6), onesel[:, :], c0hl[:, half * 512:(half + 1) * 512], False, True, ["onesel", "c0hl"], [b6])
                yield 1
                P.op("dve", lambda e, b6=b6, b7=b7, tp=tp: e.tensor_copy(out=scrB, in_=bank(b6)), reads=bk(b6), writes=["scrB"], excl=bk(b6))
                norm_rope(scrB, "scrB", scrA, "scrA", 4, 16, tq, "tq", tokbf[:, half * 512:(half + 1) * 512], "tokbf")
                yield 2
            yield 6
            b6, b7 = side_pair[0]
            tp = tr8(tokbf, "tokbf", 8, b7)
            yield 1
            P.op("dve", lambda e, b6=b6, b7=b7, tp=tp: e.tensor_copy(out=QT[par].rearrange("p h t -> p (h t)"), in_=tp), reads=bk(b7), writes=["QT%d" % par],
                 excl=bk(b7))
            for half in range(2):
                b6, b7 = side_pair[0]
                for kc in range(8):
                    mm(bank(b6), xnT[:, kc, :], W0qg[:, kc, 1024 + half * 512:1024 + (half + 1) * 512], kc == 0, False, ["xnT", "W0qg"], [b6])
                mm(bank(b6), onesel[:, :], c0hl[:, 1536 + half * 512:1536 + (half + 1) * 512], False, True, ["onesel", "c0hl"], [b6])
                yield 2
                P.op("act", lambda e, b6=b6, b7=b7, tp=tp: e.activation(out=thb, in_=bank(b6), func=AF.Tanh, scale=0.5), reads=bk(b6), writes=["thb"], excl=bk(b6))
                P.op("dve", lambda e, half=half, b6=b6, b7=b7, tp=tp: e.scalar_tensor_tensor(out=sg[par][:, half * 512:(half + 1) * 512], in0=thb, scalar=1.0,
                                                                        in1=bank(b6), op0=ALU.add, op1=ALU.mult),
                     reads=["thb"] + bk(b6), writes=["sg%d" % par], excl=bk(b6))
                yield 1

        side_pair = [(6, 7)]

        def finalize(blk, g):
            par = blk["k"] % 2
            ob = 4 if g == 0 else 6
            rs0 = 40 + 8 * par + 4 * g
            P.op("dve", lambda e, ob=ob: e.tensor_copy(out=small[:, rs0:rs0 + 3],
                                                in_=bank(ob)[:, 0:387].rearrange("p (h d) -> p h d", d=129)[:, :, 128]),
                 reads=bk(ob), writes=["rs%d" % par], excl=bk(ob))
            P.op("dve", lambda e, ob=ob: e.tensor_copy(out=small[:, rs0 + 3:rs0 + 4], in_=bank(ob + 1)[:, 128:129]), reads=bk(ob + 1), writes=["rs%d" % par],
                 excl=bk(ob + 1))
            h0 = 4 * g
            P.op("dve", lambda e, ob=ob: e.tensor_tensor(
                out=AO[:, h0 * 128:(h0 + 3) * 128].rearrange("p (h d) -> p h d", h=3),
                in0=bank(ob)[:, 0:387].rearrange("p (h d) -> p h d", d=129)[:, :, 0:128],
                in1=sg[par][:, h0 * 128:(h0 + 3) * 128].rearrange("p (h d) -> p h d", h=3), op=ALU.mult),
                reads=["sg%d" % par] + bk(ob), writes=["AO"], excl=bk(ob))
            P.op("dve", lambda e, ob=ob: e.tensor_tensor(out=AO[:, (h0 + 3) * 128:(h0 + 4) * 128], in0=bank(ob + 1)[:, 0:128],
                                                  in1=sg[par][:, (h0 + 3) * 128:(h0 + 4) * 128], op=ALU.mult),
                 reads=["sg%d" % par] + bk(ob + 1), writes=["AO"], excl=bk(ob + 1))

        def attention(blk, gens):
            par = blk["k"] % 2
            total = NKV * NPAIR
            started = set()

            def qk(idx, t):
                g, j = divmod(idx, NPAIR)
                b = idx % 2
                st = 2 * j + t
                mm(bank(2 * b + t), KT[:, g, st * 128:(st + 1) * 128],
                   QT[par][:, 4 * g:4 * g + 4, :].rearrange("p h t -> p (h t)"), True, True, ["QT%d" % par, ("KT", st)], [2 * b + t])

            def ex(idx):
                b = idx % 2
                for t in range(2):
                    P.op("act", lambda e, t=t: e.activation(out=PT[b][:, t * 512:(t + 1) * 512], in_=bank(2 * b + t), func=AF.Exp,
                                                            bias=negM[:, 0:1], scale=1.0),
                         reads=["negM"] + bk(2 * b + t), writes=["PT%d_%d" % (b, t)], excl=bk(2 * b + t))

            def pv(idx, t):
                g, j = divmod(idx, NPAIR)
                b = idx % 2
                st = 2 * j + t
                for hh in range(4):
                    bnk = o_bank(hh, g)
                    first = (g, bnk) not in started
                    started.add((g, bnk))
                    mm(o_ap(hh, 129, g), PT[b][:, t * 512 + hh * 128:t * 512 + (hh + 1) * 128], VA[:, st, g, 0:129],
                       first, False, ["PT%d_%d" % (b, t), ("VA", st), "VAones"], [bnk], skip=True)

            qk(0, 0)
            qk(0, 1)
            for idx in range(total):
                ex(idx)
                for t in range(2):
                    if idx + 1 < total:
                        qk(idx + 1, t)
                    pv(idx, t)
                if idx % NPAIR == NPAIR - 1:
                    finalize(blk, idx // NPAIR)
                side_pair[0] = (6, 7) if ((idx + 1) // NPAIR) % NKV == 0 else (4, 5)
                step_all(gens)
                if NPAIR >= 8 and idx % 16 == 9:
                    step_all(gens)

        def back_gen(blk, prev):
            k = blk["k"]
            xi = k % 3
            xk = "xt%d" % xi
            ub = blk["ub"]
            uk = "UT%d" % ub
            b6, b7 = side_pair[0]
            tp = None
            if NPAIR >= 8:
                yield 1
            par = k % 2
            P.op("dve", lambda e, b6=b6, b7=b7, tp=tp: e.reciprocal(out=small[:, 56:64], in_=small[:, 40 + 8 * par:48 + 8 * par]), reads=["rs%d" % par], writes=["rec"])
            for h in range(8):
                P.op("dve", lambda e, h=h, b6=b6, b7=b7, tp=tp: e.tensor_scalar(out=AO[:, h * 128:(h + 1) * 128], in0=AO[:, h * 128:(h + 1) * 128],
                                                           scalar1=small[:, 56 + h:57 + h], scalar2=None, op0=ALU.mult),
                     reads=["rec", "AO"], writes=["AO"])
            if NPAIR >= 8:
                yield 2
            b6, b7 = side_pair[0]
            tp = tr8(AO, "AO", 8, b7)
            yield 1
            P.op("dve", lambda e, b6=b6, b7=b7, tp=tp: e.tensor_copy(out=xnT.rearrange("p k t -> p (k t)"), in_=tp), reads=bk(b7), writes=["xnT"], excl=bk(b7))
            yield 1
            for half in range(2):
                b6, b7 = side_pair[0]
                for hc in range(8):
                    mm(bank(b6), xnT[:, hc, :], Wo0[:, hc, half * 512:(half + 1) * 512], hc == 0, hc == 7, ["xnT", "Wo0"], [b6])
                yield 1
                P.op("dve", lambda e, half=half, b6=b6, b7=b7, tp=tp: e.tensor_tensor(out=xt[xi][:, half * 512:(half + 1) * 512],
                                                                 in0=xt[xi][:, half * 512:(half + 1) * 512], in1=bank(b6), op=ALU.add),
                     reads=[xk] + bk(b6), writes=[xk], excl=bk(b6))
            rms_scale(xt[xi], xk, tokbf, "tokbf", scrA.bitcast(BF16)[:, 0:D], "scrA", "dve", 8)
            yield 6
            b6, b7 = side_pair[0]
            tp = tr8(tokbf, "tokbf", 8, b7)
            yield 1
            P.op("dve", lambda e, b6=b6, b7=b7, tp=tp: e.tensor_copy(out=xnT.rearrange("p k t -> p (k t)"), in_=tp), reads=bk(b7), writes=["xnT"], excl=bk(b7))
            yield 1
            for half in range(2):
                b6, b7 = side_pair[0]
                for j4 in range(4):
                    j = half * 4 + j4
                    for kc in range(8):
                        mm(bank(b6)[:, j4 * 128:(j4 + 1) * 128], W1[:, kc, j * 128:(j + 1) * 128], xnT[:, kc, :], kc == 0, kc == 7,
                           ["xnT", "W1"], [b6])
                yield 1
                if blk["halo"]:
                    P.op("dve", lambda e, half=half, b6=b6, b7=b7, tp=tp: e.tensor_tensor(
                        out=UH[:, half * 4:(half + 1) * 4, :], in0=bank(b6).rearrange("p (k t) -> p k t", k=4)[:, :, 0:16],
                        in1=c1col[:, half * 4:(half + 1) * 4].unsqueeze(2).to_broadcast([128, 4, 16]), op=ALU.add),
                        reads=["c1col"] + bk(b6), writes=["UH"], excl=bk(b6))
                else:
                    P.op("dve", lambda e, half=half, b6=b6, b7=b7, tp=tp: e.tensor_tensor(
                        out=UT[ub][:, half * 4:(half + 1) * 4, 8:136], in0=bank(b6).rearrange("p (k t) -> p k t", k=4),
                        in1=c1col[:, half * 4:(half + 1) * 4].unsqueeze(2).to_broadcast([128, 4, 128]), op=ALU.add),
                        reads=["c1col"] + bk(b6), writes=[uk], excl=bk(b6))
            if blk["halo"]:
                P.op("dve", lambda e, b6=b6, b7=b7, tp=tp: e.tensor_tensor(out=UH[:, :, :], in0=UH[:, :, :],
                                                      in1=maskh[:, :].unsqueeze(1).to_broadcast([128, 8, 16]), op=ALU.mult),
                     reads=["UH", "maskh"], writes=["UH"])
                return
            for half in range(2):
                b6, b7 = side_pair[0]
                for j4 in range(4):
                    j = half * 4 + j4
                    for kc in range(8):
                        mm(bank(b7)[:, j4 * 128:(j4 + 1) * 128], W1[:, kc, D + j * 128:D + (j + 1) * 128], xnT[:, kc, :], kc == 0, kc == 7,
                           ["xnT", "W1"], [b7])
                yield 1
                g3 = scrB.rearrange("p (k t) -> p k t", k=4)
                P.op("dve", lambda e, half=half, b6=b6, b7=b7, tp=tp: e.tensor_tensor(
                    out=g3, in0=bank(b7).rearrange("p (k t) -> p k t", k=4),
                    in1=c1col[:, 8 + half * 4:8 + (half + 1) * 4].unsqueeze(2).to_broadcast([128, 4, 128]), op=ALU.add),
                    reads=["c1col"] + bk(b7), writes=["scrB"], excl=bk(b7))
                yield 2
                P.op("act", lambda e, b6=b6, b7=b7, tp=tp: e.activation(out=scrA[:, 0:512], in_=scrB, func=AF.Tanh, scale=0.5), reads=["scrB"], writes=["scrA"])
                P.op("dve", lambda e, half=half, b6=b6, b7=b7, tp=tp: e.scalar_tensor_tensor(
                    out=sg1T[ub][:, half * 4:(half + 1) * 4, :].rearrange("p k t -> p (k t)"), in0=scrA[:, 0:512], scalar=1.0, in1=scrB,
                    op0=ALU.add, op1=ALU.mult), reads=["scrA", "scrB"], writes=["sg1T%d" % ub])
                yield 1
            if blk["first"]:
                if blk["slot"] == 0:
                    P.op("pool", lambda e, b6=b6, b7=b7, tp=tp: e.memset(UT[ub][:, :, 0:8], 0.0), writes=[uk])
                else:
                    P.op("pool", lambda e, b6=b6, b7=b7, tp=tp: e.tensor_copy(out=UT[ub][:, :, 0:8], in_=UH[:, :, 0:8]), reads=["UH"], writes=[uk])
            else:
                pk = "UT%d" % (1 - ub)
                P.op("pool", lambda e, b6=b6, b7=b7, tp=tp: e.tensor_copy(out=UT[ub][:, :, 0:8], in_=UT[1 - ub][:, :, 128:136]), reads=[pk], writes=[uk])
                P.op("pool", lambda e, b6=b6, b7=b7, tp=tp: e.tensor_copy(out=UT[1 - ub][:, :, 136:144], in_=UT[ub][:, :, 8:16]), reads=[uk], writes=[pk])
            if prev is not None and not prev["halo"]:
                yield 1
                yield from l1back_gen(prev)

        def l1back_gen(pb):
            k = pb["k"]
            xi = k % 3
            xk = "xt%d" % xi
            ub = pb["ub"]
            uk = "UT%d" % ub
            U = UT[ub]
            b6, b7 = side_pair[0]
            tp = None
            TA = scrA[:, 0:288].rearrange("p (c n) -> p c n", c=2)
            TB = scrA[:, 288:576].rearrange("p (c n) -> p c n", c=2)
            mixT = tokbf.rearrange("p (k t) -> p k t", k=8)
            ci = pb["corr"]
            if ci is not None:
                corr = scrB.rearrange("p (g t) -> p g t", g=4)
                P.dma(lambda e, b6=b6, b7=b7, tp=tp: e.dma_start(out=scrB, in_=corr_d[0:1, ci * 512:(ci + 1) * 512].partition_broadcast(128)),
                      writes=["scrB"], chan="corr")

            def add(out, a, b_, rk):
                P.op("dve", lambda e, b6=b6, b7=b7, tp=tp: e.tensor_tensor(out=out, in0=a, in1=b_, op=ALU.add), reads=rk + ["scrA"], writes=["scrA"])

            for g in range(4):
                c0, c1 = 2 * g, 2 * g + 2
                w = WINS[g]
                if g == 0:
                    add(TB[:, :, 0:128], U[:, c0:c1, 7:135], U[:, c0:c1, 8:136], [uk])
                elif g == 1:
                    add(TA[:, :, 0:130], U[:, c0:c1, 6:136], U[:, c0:c1, 7:137], [uk])
                    add(TB[:, :, 0:128], TA[:, :, 0:128], TA[:, :, 2:130], [])
                elif g == 2:
                    add(TA[:, :, 0:135], U[:, c0:c1, 4:139], U[:, c0:c1, 5:140], [uk])
                    add(TB[:, :, 0:132], TA[:, :, 0:132], TA[:, :, 2:134], [])
                    add(TA[:, :, 0:128], TB[:, :, 0:128], TB[:, :, 4:132], [])
                else:
                    add(TA[:, :, 0:142], U[:, c0:c1, 0:142], U[:, c0:c1, 1:143], [uk])
                    add(TB[:, :, 0:140], TA[:, :, 0:140], TA[:, :, 2:142], [])
                    add(TA[:, :, 0:136], TB[:, :, 0:136], TB[:, :, 4:140], [])
                    add(TB[:, :, 0:128], TA[:, :, 0:128], TA[:, :, 8:136], [])
                res = TA if g == 2 else TB
                if ci is not None:
                    P.op("dve", lambda e, res=res, g=g, b6=b6, b7=b7, tp=tp: e.tensor_tensor(
                        out=res[:, :, 0:128], in0=res[:, :, 0:128],
                        in1=corr[:, g, :].unsqueeze(1).to_broadcast([128, 2, 128]), op=ALU.mult),
                        reads=["scrA", "scrB"], writes=["scrA"])
                P.op("dve", lambda e, res=res, c0=c0, c1=c1, w=w, b6=b6, b7=b7, tp=tp: e.scalar_tensor_tensor(
                    out=mixT[:, c0:c1, :], in0=res[:, :, 0:128], scalar=1.0 / w, in1=U[:, c0:c1, 8:136],
                    op0=ALU.mult, op1=ALU.subtract), reads=["scrA", uk], writes=["tokbf"])
                if g % 2 == 1:
                    yield 1
            yield 5
            for half in range(2):
                b6, b7 = side_pair[0]
                for gg in range(2):
                    g = half * 2 + gg
                    for dc in range(2):
                        j4 = gg * 2 + dc
                        for cc in range(2):
                            mm(bank(b6)[:, j4 * 128:(j4 + 1) * 128], Wg[:, g * 2 + cc, dc * 128:(dc + 1) * 128],
                               mixT[:, g * 2 + cc, :], cc == 0, cc == 1, ["tokbf", "Wg"], [b6])
                yield 1
                P.op("dve", lambda e, half=half, b6=b6, b7=b7, tp=tp: e.tensor_tensor(
                    out=xnT[:, half * 4:(half + 1) * 4, :].rearrange("p k t -> p (k t)"), in0=bank(b6),
                    in1=sg1T[ub][:, half * 4:(half + 1) * 4, :].rearrange("p k t -> p (k t)"), op=ALU.mult),
                    reads=["sg1T%d" % ub] + bk(b6), writes=["xnT"], excl=bk(b6))
            yield 1
            for half in range(2):
                b6, b7 = side_pair[0]
                for fc in range(8):
                    mm(bank(b7), xnT[:, fc, :], Wo1[:, fc, half * 512:(half + 1) * 512], fc == 0, fc == 7, ["xnT", "Wo1"], [b7])
                yield 1
                P.op("dve", lambda e, half=half, b6=b6, b7=b7, tp=tp: e.tensor_tensor(out=xt[xi][:, half * 512:(half + 1) * 512],
                                                                 in0=xt[xi][:, half * 512:(half + 1) * 512], in1=bank(b7), op=ALU.add),
                     reads=[xk] + bk(b7), writes=[xk], excl=bk(b7))
            P.dma(lambda e, b6=b6, b7=b7, tp=tp: e.dma_start(out=pb["ydst"], in_=xt[xi]), reads=[xk], chan="y%d" % xi, final=True)

        def q_phase(s, kbase):
            blocks = []
            if s == 0:
                for i in range(NT):
                    blocks.append(dict(slot=0, halo=False, first=(i == 0), x=xs_d[0][i * 128:(i + 1) * 128, :],
                                       rope=rope_d[i * 128:(i + 1) * 128, :], ydst=y_d[0][i * 128:(i + 1) * 128, :], ub=i % 2,
                                       corr=(0 if i == 0 else (1 if i == NT - 1 else None))))
            else:
                blocks.append(dict(slot=1, halo=True, first=False, x=xh_d[:, :], rope=ropeh_d[:, :], ydst=None, ub=0, corr=None))
                for i in range(NB1):
                    blocks.append(dict(slot=1, halo=False, first=(i == 0), x=xq1_d[i * 128:(i + 1) * 128, :],
                                       rope=ropeq1_d[i * 128:(i + 1) * 128, :], ydst=y_d[1][i * 128:(i + 1) * 128, :], ub=i % 2,
                                       corr=(2 if i == 0 else (3 if i == NB1 - 1 else None))))
            for n, b in enumerate(blocks):
                b["k"] = kbase + n
            nb = len(blocks)

            def load_front(b):
                xi = (b["k"] - 1) % 3
                P.dma(lambda e: e.dma_start(out=xt[xi], in_=b["x"]), writes=["xt%d" % xi], chan="xt%d" % xi)

            def load_back(b):
                xi = b["k"] % 3
                P.dma(lambda e: e.dma_start(out=xt[xi], in_=b["x"]), writes=["xt%d" % xi], chan="xt%d" % xi)

            load_front(blocks[0])
            drain([[front_gen(blocks[0]), 0]])
            for n, b in enumerate(blocks):
                def side():
                    if n >= 1:
                        pv_ = blocks[n - 1]
                        pp = blocks[n - 2] if n >= 2 else None
                        yield from back_gen(pv_, pp)
                    if n + 1 < nb:
                        yield from front_gen(blocks[n + 1])
                if n >= 1:
                    load_back(blocks[n - 1])
                if n + 1 < nb:
                    load_front(blocks[n + 1])
                gens = [[side(), 0]]
                attention(b, gens)
                nleft = 0
                while gens:
                    step_all(gens)
                    nleft += 1
                if n == 1:
                    print("[build] slot %d: side steps left after attention: %d" % (s, nleft))
            last = blocks[-1]
            load_back(last)
            drain([[back_gen(last, blocks[-2] if nb >= 2 else None), 0]])
            ub = last["ub"]
            uk = "UT%d" % ub
            if s == 0:
                P.op("pool", lambda e: e.memset(UT[ub][:, :, 136:144], 0.0), writes=[uk])
            else:
                P.op("pool", lambda e: e.tensor_copy(out=UT[ub][:, :, 136:144], in_=UH[:, :, 8:16]), reads=["UH"], writes=[uk])
            drain([[l1back_gen(last), 0]])
            return nb

        kbase = 0
        mod_pass()
        for s in range(2):
            slot_prep(s)
            kv_phase(s)
            P.fence()
            kbase += q_phase(s, kbase)

        sems = {e: es.enter_context(nc.semaphore("sem_" + e)) for e in ENGS}
        dsems = {c: es.enter_context(nc.semaphore("dsem_" + c)) for c in P.channels()}
        with nc.Block() as block:
            P.emit(block, sems, dsems)
    return nc


def host_prep(inp, S):
    f32 = np.float32
    QW = S // 4
    xsamp = np.asarray(inp["x_sample"], f32)
    xprom = np.asarray(inp["x_prompt"], f32)
    csamp = np.asarray(inp["c_sample"], f32)
    cprom = np.asarray(inp["c_prompt"], f32)
    norm_g = np.asarray(inp["norm_g"], f32)
    ada_w = np.ascontiguousarray(np.asarray(inp["ada_w"], f32))
    ada_b = np.ascontiguousarray(np.asarray(inp["ada_b"], f32))
    pos = np.arange(S)
    rowp = (pos // GRID_W).astype(f32)
    colp = (pos % GRID_W).astype(f32)
    inv = (f32(10000.0) ** (-(np.arange(0, HD // 2, 2, dtype=f32)) / f32(HD // 2))).astype(f32)
    ang = np.concatenate([rowp[:, None] * inv[None, :], colp[:, None] * inv[None, :]], axis=-1).astype(f32)
    rope = np.concatenate([np.cos(ang), np.sin(ang)], axis=-1).astype(f32)

    def corr_for(tpos):
        out = np.ones((4, 128), f32)
        for j, w in enumerate(WINS):
            lo = np.clip(tpos - w // 2, 0, S)
            hi = np.clip(tpos - w // 2 + w, 0, S)
            out[j] = w / (hi - lo).astype(f32)
        return out

    common = {
        "rope": rope,
        "gcol": np.ascontiguousarray(norm_g.reshape(2, 8, 128).transpose(2, 0, 1).reshape(128, 16)),
        "abcol": np.ascontiguousarray(np.repeat(ada_b[:, 0:2048].reshape(2, 16, 128).transpose(2, 0, 1).reshape(128, 32), 2, axis=1)),
        "ada_b": ada_b,
        "ada_w": ada_w,
        "w_in0": np.ascontiguousarray(np.asarray(inp["attn_w_in"], f32)[0]),
        "w_out0": np.ascontiguousarray(np.asarray(inp["attn_w_out"], f32)[0]),
        "w_in1": np.ascontiguousarray(np.asarray(inp["pool_w_in"], f32)[0]),
        "w_grp": np.ascontiguousarray(np.asarray(inp["pool_w_group"], f32)[0].reshape(1024, 256)),
        "w_out1": np.ascontiguousarray(np.asarray(inp["pool_w_out"], f32)[0]),
        "qg": np.ascontiguousarray(np.asarray(inp["attn_q_norm"], f32)[0:1]),
        "kg": np.ascontiguousarray(np.asarray(inp["attn_k_norm"], f32)[0:1]),
        "pscale": np.ascontiguousarray(np.asarray(inp["pool_scale"], f32)[0:1]),
    }
    in_maps = []
    for c in range(8):
        pb, r = c // 4, c % 4
        q0 = r * QW
        xs1 = xprom[pb]
        xh = np.zeros((128, D), f32)
        hpos = np.zeros(128, np.int64)
        maskh = np.zeros((1, 16), f32)
        if r > 0:
            xh[0:8] = xs1[q0 - 8:q0]
            hpos[0:8] = np.arange(q0 - 8, q0)
            maskh[0, 0:8] = 1.0
        if r < 3:
            xh[8:16] = xs1[q0 + QW:q0 + QW + 8]
            hpos[8:16] = np.arange(q0 + QW, q0 + QW + 8)
            maskh[0, 8:16] = 1.0
        cvec = np.stack([csamp[c], cprom[pb]], 0)
        cT = np.ascontiguousarray(cvec.reshape(2, 8, 128).transpose(2, 1, 0).reshape(128, 16))
        corr = np.stack([corr_for(np.arange(0, 128)), corr_for(np.arange(S - 128, S)),
                         corr_for(np.arange(q0, q0 + 128)), corr_for(np.arange(q0 + QW - 128, q0 + QW))], 0)
        m = dict(common)
        m.update({
            "xs0": np.ascontiguousarray(xsamp[c]),
            "xs1": np.ascontiguousarray(xs1),
            "xq1": np.ascontiguousarray(xs1[q0:q0 + QW]),
            "xh": xh,
            "cT": cT,
            "ropeq1": np.ascontiguousarray(rope[q0:q0 + QW]),
            "ropeh": np.ascontiguousarray(rope[hpos]),
            "corr": np.ascontiguousarray(corr.reshape(1, -1)),
            "maskh": maskh,
        })
        in_maps.append(m)
    return in_maps


_NC_CACHE = {}


def run(inputs, S):
    if S not in _NC_CACHE:
        _NC_CACHE[S] = build_program(S)
    nc = _NC_CACHE[S]
    in_maps = host_prep(inputs, S)
    res = run_bass_kernel_spmd(nc, in_maps, core_ids=list(range(8)))
    QW = S // 4
    y_sample = np.stack([np.asarray(res.results[c]["y0"], np.float32) for c in range(8)], 0)
    y_prompt = np.zeros((2, S, D), np.float32)
    for c in range(8):
        y_prompt[c // 4, (c % 4) * QW:(c % 4 + 1) * QW] = np.asarray(res.results[c]["y1"], np.float32)
    return y_prompt, y_sample


def kernel(**inputs):
    S = int(np.asarray(inputs["x_sample"]).shape[1])
    return run(inputs, S)
```

```python
import numpy as np
from contextlib import ExitStack
import concourse.bass as bass
import concourse.mybir as mybir
from concourse.bass_utils import run_bass_kernel_spmd

F32 = mybir.dt.float32
BF16 = mybir.dt.bfloat16
AF = mybir.ActivationFunctionType
ALU = mybir.AluOpType
AX = mybir.AxisListType

D = 1024
NCH = 8
HD = 128
NH = 8
NKV = 2
GRID_W = 64
WINS = (2, 4, 8, 16)
EPS = 1e-6
ENGS = ("pe", "act", "dve", "pool", "sp")


class _Op:
    __slots__ = ("eng", "build", "deps", "idx", "sig", "chan", "val", "is_dma")

    def __init__(self, eng, build, deps, is_dma, chan):
        self.eng, self.build, self.deps, self.is_dma, self.chan = eng, build, deps, is_dma, chan
        self.sig = False
        self.val = None
        self.idx = None


class Prog:
    def __init__(self):
        self.ops = {e: [] for e in ENGS}
        self.lastw = {}
        self.readers = {}
        self.lastacc = {}
        self.final = []
        self.chan_count = {}
        self.fence_deps = []
        self.last_dma = {}

    @staticmethod
    def _stream(op):
        return ("dma", op.chan) if op.is_dma else op.eng

    def op(self, eng, build, reads=(), writes=(), excl=(), chan=None, extra=()):
        is_dma = chan is not None
        deps = {}

        def add(d):
            if d is None:
                return
            if (not is_dma) and (not d.is_dma) and d.eng == "pe" and eng == "pe":
                return
            s = self._stream(d)
            cur = deps.get(s)
            if cur is None or d.idx > cur.idx:
                deps[s] = d

        for k in list(reads) + list(writes):
            add(self.lastw.get(k))
        for k in writes:
            for r in self.readers.get(k, {}).values():
                add(r)
        for k in excl:
            for e2, o2 in self.lastacc.get(k, {}).items():
                if e2 != eng:
                    add(o2)
        for d in extra:
            add(d)
        for d in self.fence_deps:
            add(d)
        o = _Op(eng, build, list(deps.values()), is_dma, chan)
        if is_dma:
            c = self.chan_count.get(chan, 0) + 1
            self.chan_count[chan] = c
            o.idx = c
            self.last_dma[chan] = o
        else:
            o.idx = len(self.ops[eng])
        self.ops[eng].append(o)
        for k in writes:
            self.lastw[k] = o
            self.readers[k] = {}
        for k in reads:
            self.readers.setdefault(k, {})[self._stream(o)] = o
        for k in excl:
            self.lastacc.setdefault(k, {})[eng] = o
        return o

    def dma(self, build, reads=(), writes=(), chan=None, final=False, extra=()):
        o = self.op("sp", build, reads, writes, chan=chan, extra=extra)
        if final:
            self.final.append(o)
        return o

    def fence(self):
        deps = []
        for e in ENGS:
            for o in reversed(self.ops[e]):
                if not o.is_dma:
                    deps.append(o)
                    break
        deps.extend(self.last_dma.values())
        self.fence_deps = deps

    def channels(self):
        return sorted(self.chan_count.keys())

    def emit(self, block, sems, dma_sems):
        for e in ENGS:
            for o in self.ops[e]:
                for d in o.deps:
                    d.sig = True
        for o in self.final:
            o.sig = True
        for e in ENGS:
            cnt = 0
            for o in self.ops[e]:
                if o.is_dma:
                    o.val = 16 * o.idx
                    o.sig = True
                elif o.sig:
                    cnt += 1
                    o.val = cnt

        def semof(d):
            return dma_sems[d.chan] if d.is_dma else sems[d.eng]

        def run(e, engobj):
            seen = {}
            for o in self.ops[e]:
                for d in o.deps:
                    s = self._stream(d)
                    if seen.get(s, 0) >= d.val:
                        continue
                    seen[s] = d.val
                    engobj.wait_ge(semof(d), d.val)
                ins = o.build(engobj)
                if o.sig:
                    if o.is_dma:
                        ins.then_inc(dma_sems[o.chan], 16)
                    else:
                        ins.then_inc(sems[e], 1)
            if e == "sp":
                for o in self.final:
                    s = self._stream(o)
                    if seen.get(s, 0) >= o.val:
                        continue
                    seen[s] = o.val
                    engobj.wait_ge(semof(o), o.val)

        @block.tensor
        def _(eng):
            run("pe", eng)

        @block.scalar
        def _(eng):
            run("act", eng)

        @block.vector
        def _(eng):
            run("dve", eng)

        @block.gpsimd
        def _(eng):
            run("pool", eng)

        @block.sync
        def _(eng):
            run("sp", eng)


def build_program(S, dbg=False):
    NT = S // 128
    QW = S // 4
    NB1 = QW // 128
    NPAIR = NT // 2
    assert NT % 2 == 0 and NB1 >= 1

    nc = bass.Bass("TRN2", target_bir_lowering=False, dynamic_dma_scratch_size=256)
    P = Prog()

    def din(name, shape, dt=F32):
        return nc.dram_tensor(name, list(shape), dt, kind="ExternalInput").ap()

    xs_d = [din("xs0", [S, D]), din("xs1", [S, D])]
    xq1_d = din("xq1", [QW, D])
    xh_d = din("xh", [128, D])
    cT_d = din("cT", [128, 16])
    rope_d = din("rope", [S, 128])
    ropeq1_d = din("ropeq1", [QW, 128])
    ropeh_d = din("ropeh", [128, 128])
    gcol_d = din("gcol", [128, 16])
    abcol_d = din("abcol", [128, 64])
    adab_d = din("ada_b", [2, 3 * D])
    adaw_d = din("ada_w", [2, D, 3 * D])
    win0_d = din("w_in0", [D, 2560])
    wout0_d = din("w_out0", [D, D])
    win1_d = din("w_in1", [D, 2 * D])
    wgrp_d = din("w_grp", [D, 256])
    wout1_d = din("w_out1", [D, D])
    qg_d = din("qg", [1, 128])
    kg_d = din("kg", [1, 128])
    pscale_d = din("pscale", [1, D])
    corr_d = din("corr", [1, 4 * 4 * 128])
    maskh_d = din("maskh", [1, 16])
    y_d = [nc.dram_tensor("y0", [S, D], F32, kind="ExternalOutput").ap(),
           nc.dram_tensor("y1", [QW, D], F32, kind="ExternalOutput").ap()]

    es = ExitStack()
    with es:
        def sb(name, shape, dt):
            return es.enter_context(nc.sbuf_tensor("s_" + name, shape, dt))

        PS = [es.enter_context(nc.psum_tensor("p_%d" % i, [128, 1024], F32)) for i in range(4)]

        def bank(i):
            return PS[i // 2][:, (i % 2) * 512:(i % 2) * 512 + 512]

        def bk(*ids):
            return ["b%d" % i for i in ids]

        TPB = PS[3][:, 512:1024].bitcast(BF16)
        PM = PS[3][:, 512:1024]

        ident = sb("ident", [128, 128], BF16)
        onesel = sb("onesel", [64, 128], BF16)
        mhalf = sb("mhalf", [128, 1], F32)
        negM = sb("negM", [128, 1], F32)
        small = sb("small", [128, 64], F32)
        cTt = sb("cTt", [128, 16], F32)
        scT = sb("scT", [128, 16], F32)
        gcol = sb("gcol", [128, 16], F32)
        abcol = sb("abcol", [128, 64], F32)
        modc = sb("modc", [128, 64], F32)
        Acol = sb("Acol", [128, 16], F32)
        c1col = sb("c1col", [128, 16], F32)
        gtab = sb("gtab", [128, 2, 128], F32)
        maskh = sb("maskh", [128, 16], F32)
        UH = sb("UH", [128, 8, 16], F32)
        c0hl = sb("c0hl", [64, 2560], BF16)
        W0qg = sb("W0qg", [128, NCH, 2048], BF16)
        Wo0 = sb("Wo0", [128, NCH, D], BF16)
        W1 = sb("W1", [128, NCH, 2 * D], BF16)
        Wg = sb("Wg", [128, NCH, 256], BF16)
        Wo1 = sb("Wo1", [128, NCH, D], BF16)
        KT = sb("KT", [128, NKV, S], BF16)
        VA = sb("VA", [128, NT, NKV, 129], BF16)

        WORK_BYTES = 51200
        arena = sb("arena", [128, WORK_BYTES // 4], F32)
        apos = [0]

        def view(ap, dt, shape):
            if dt == BF16:
                ap = ap.bitcast(BF16)
            if len(shape) == 2:
                return ap.rearrange("p (a b) -> p a b", a=shape[0])
            if len(shape) == 3:
                return ap.rearrange("p (a b c) -> p a b c", a=shape[0], b=shape[1])
            return ap

        def carve(nbytes, dt, shape):
            assert nbytes % 4 == 0
            a = apos[0]
            apos[0] += nbytes // 4
            assert apos[0] * 4 <= WORK_BYTES, "arena overflow %d" % (apos[0] * 4)
            return view(arena[:, a:a + nbytes // 4], dt, shape)

        xt = [carve(4096, F32, [D]) for _ in range(3)]
        kvbase = apos[0]
        tokbf = carve(2048, BF16, [D])
        AO = carve(2048, BF16, [D])
        xnT = carve(2048, BF16, [NCH, 128])
        thb = carve(1024, BF16, [512])
        scrA = carve(2560, F32, [640])
        scrB = carve(2048, F32, [512])
        QT = [carve(2048, BF16, [NH, 128]) for _ in range(2)]
        sg = [carve(2048, BF16, [D]) for _ in range(2)]
        PT = [carve(2048, BF16, [D]) for _ in range(2)]
        cs = carve(512, F32, [128])
        tq = carve(1024, F32, [4, 64])
        UT = [carve(8 * 144 * 4, F32, [8, 144]) for _ in range(2)]
        sg1T = [carve(2048, BF16, [NCH, 128]) for _ in range(2)]
        q_words = apos[0]
        apos[0] = 0
        W0kv = carve(8192, BF16, [NCH, 512])
        kv_x = [carve(4096, F32, [D]) for _ in range(4)]
        kv_cs = [carve(512, F32, [128]) for _ in range(3)]
        kv_tq = [carve(1024, F32, [4, 64]) for _ in range(2)]
        kv_xn = [carve(2048, BF16, [D]) for _ in range(2)]
        kv_xnT = [carve(2048, BF16, [NCH, 128]) for _ in range(2)]
        kv_krot = [carve(512, BF16, [256]) for _ in range(2)]
        kv_junk = carve(2048, BF16, [D])
        kv_junkA = carve(512, BF16, [256])
        kv_kn = carve(1024, F32, [256])
        kv_tmp = carve(1024, F32, [256])
        kv_words = apos[0]
        apos[0] = max(q_words, kv_words)

        kt_words = NKV * S // 2
        va_words = (NT * NKV * 129) // 2
        KTf = KT.reshape([128, NKV * S])[:, :].bitcast(F32)
        VAf = VA.reshape([128, NT * NKV * 129])[:, 0:2 * va_words].bitcast(F32)
        if kt_words >= 8192 and va_words >= 7680:
            reg_kt, reg_va = KTf, VAf
        else:
            reg_kt = sb("prep_a", [128, 4096], F32)[:, :]
            reg_va = sb("prep_b", [128, 7680], F32)[:, :]
        adaw_bufs = 2 if reg_kt.shape[1] >= 8192 else 1
        ar_off = 2048
        reg_ar = arena[:, ar_off:WORK_BYTES // 4]
        ut_off = WORK_BYTES // 4 - ar_off
        assert ut_off >= 10240, ut_off

        def mm(out, lhsT, rhs, start, stop, reads, banks, skip=False):
            return P.op("pe", lambda e: e.matmul(out, lhsT=lhsT, rhs=rhs, start=start, stop=stop,
                                                 skip_group_check=skip),
                        reads=reads, writes=bk(*banks), excl=bk(*banks))

        def transposes(src, n, reads):
            for i in range(n):
                P.op("pe", lambda e, i=i: e.transpose(out=TPB[:, i * 128:(i + 1) * 128],
                                                      in_=src[:, i * 128:(i + 1) * 128], identity=ident[:, :]),
                     reads=list(reads) + ["ident"], writes=bk(7), excl=bk(7))

        iot = scrA[:, 0:128]
        P.op("pool", lambda e: e.iota(iot, [[1, 128]], base=0, channel_multiplier=-1,
                                      allow_small_or_imprecise_dtypes=True), writes=["scrA"])
        P.op("dve", lambda e: e.tensor_single_scalar(out=ident[:, :], in_=iot, scalar=0.0, op=ALU.is_equal),
             reads=["scrA"], writes=["ident"])
        P.op("dve", lambda e: e.memset(onesel[:, :], 0.0), writes=["onesel"])
        P.op("dve", lambda e: e.memset(onesel[0:1, :], 1.0), writes=["onesel"])
        P.op("dve", lambda e: e.memset(onesel[32:33, :], 1.0), writes=["onesel"])
        P.op("dve", lambda e: e.memset(mhalf[:, :], -0.5), writes=["mhalf"])
        P.op("dve", lambda e: e.memset(c0hl[:, :], 0.0), writes=["c0hl"])
        P.dma(lambda e: e.dma_start(out=cTt[:, :], in_=cT_d[:, :]), writes=["cTt"], chan="c_cT")
        P.dma(lambda e: e.dma_start(out=gcol[:, :], in_=gcol_d[:, :]), writes=["gcol"], chan="c_gcol")
        P.dma(lambda e: e.dma_start(out=abcol[:, :], in_=abcol_d[:, :]), writes=["abcol"], chan="c_abcol")
        P.dma(lambda e: e.dma_start(out=gtab[:, 0, :], in_=qg_d[0:1, :].partition_broadcast(128)), writes=["gtab"], chan="c_qg")
        P.dma(lambda e: e.dma_start(out=gtab[:, 1, :], in_=kg_d[0:1, :].partition_broadcast(128)), writes=["gtab"], chan="c_kg")
        P.dma(lambda e: e.dma_start(out=maskh[:, :], in_=maskh_d[0:1, :].partition_broadcast(128)), writes=["maskh"], chan="c_maskh")
        P.op("dve", lambda e: e.tensor_reduce(out=small[:, 0:2], in_=gtab[:, :, :], axis=AX.X, op=ALU.max,
                                              apply_absolute_value=True), reads=["gtab"], writes=["small01"])
        P.op("dve", lambda e: e.tensor_tensor(out=small[:, 2:3], in0=small[:, 0:1], in1=small[:, 1:2], op=ALU.mult),
             reads=["small01"], writes=["small2"])
        P.op("dve", lambda e: e.tensor_scalar(out=negM[:, :], in0=small[:, 2:3], scalar1=-float(np.sqrt(128.0)),
                                              scalar2=None, op0=ALU.mult), reads=["small2"], writes=["negM"])
        P.op("dve", lambda e: e.tensor_scalar(out=gtab[:, 0, :], in0=gtab[:, 0, :], scalar1=float(HD ** -0.5),
                                              scalar2=None, op0=ALU.mult), reads=["small01"], writes=["gtab"])
        P.op("act", lambda e: e.activation(out=scT[:, :], in_=cTt[:, :], func=AF.Tanh, scale=0.5),
             reads=["cTt"], writes=["scT"])
        P.op("dve", lambda e: e.scalar_tensor_tensor(out=scT[:, :], in0=scT[:, :], scalar=1.0, in1=cTt[:, :],
                                                     op0=ALU.add, op1=ALU.mult), reads=["scT", "cTt"], writes=["scT"])
        P.op("dve", lambda e: e.tensor_scalar(out=scT[:, :], in0=scT[:, :], scalar1=0.5, scalar2=None, op0=ALU.mult),
             reads=["scT"], writes=["scT"])

        gsc_d = nc.dram_tensor("gsc", [2, 128, D], F32).ap()

        def staging():
            st_adaw = [reg_kt[:, (i % adaw_bufs) * 4096:(i % adaw_bufs + 1) * 4096].rearrange("p (k f) -> p k f", k=8) for i in range(2)]
            if adaw_bufs == 2:
                st_adaw.append(reg_va[:, 0:4096].rearrange("p (k f) -> p k f", k=8))
            st_w = [reg_va[:, i * 2560:(i + 1) * 2560] for i in range(2)]
            if adaw_bufs == 2:
                st_w += [reg_kt[:, i * 2560:(i + 1) * 2560] for i in range(3)]
            c0f = reg_va[0:64, 5120:7680]
            o = 0
            hi32 = reg_ar[0:64, o:o + 2560]
            o += 2560
            scbc = reg_ar[:, o:o + 2048].rearrange("p (s k f) -> p s k f", s=2, k=8)
            o += 2048
            gaterow = [[reg_ar[:, o + (s2 * 2 + l) * 1024: o + (s2 * 2 + l + 1) * 1024] for l in range(2)] for s2 in range(2)]
            o += 4096
            abrow = reg_ar[:, o:o + 512]
            o += 512
            psrow = reg_ar[:, o:o + 1024]
            o += 1024
            assert o <= ut_off
            return st_adaw, st_w, c0f, hi32, scbc, gaterow, abrow, psrow

        def mod_pass():
            P.fence()
            st_adaw, st_w, c0f, hi32, scbc, gaterow, abrow, psrow = staging()
            P.op("dve", lambda e: e.memset(scbc, 1.0), writes=["scbc"])
            for s2 in range(2):
                for kc in range(8):
                    P.op("dve", lambda e, kc=kc, s2=s2: e.tensor_scalar(out=scbc[:, s2, kc, :], in0=scbc[:, s2, kc, :],
                                                                        scalar1=scT[:, kc * 2 + s2:kc * 2 + s2 + 1], scalar2=None,
                                                                        op0=ALU.mult), reads=["scT"], writes=["scbc"])
            nst = 0
            for l in range(2):
                for cb in range(6):
                    nb_ = len(st_adaw) if adaw_bufs == 2 else 1
                    buf = st_adaw[nst % nb_]
                    key = "st_adaw%d" % (nst % nb_)
                    nst += 1
                    P.dma(lambda e, buf=buf, l=l, cb=cb: e.dma_start(
                        out=buf, in_=adaw_d[l, :, cb * 512:(cb + 1) * 512].rearrange("(k p) f -> p k f", p=128)),
                        writes=[key], chan=key)
                    if cb < 4:
                        for j4 in range(4):
                            j = cb * 4 + j4
                            c2 = (l * 16 + j) * 2
                            for kc in range(8):
                                mm(PM[:, c2:c2 + 2], buf[:, kc, j4 * 128:(j4 + 1) * 128], scT[:, kc * 2:kc * 2 + 2],
                                   kc == 0, kc == 7, [key, "scT"], [7])
                    else:
                        half = cb - 4
                        P.dma(lambda e, l=l, half=half: e.dma_start(
                            out=abrow, in_=adab_d[l:l + 1, 2048 + half * 512:2048 + (half + 1) * 512].partition_broadcast(128)),
                            writes=["abrow"], chan="abrow")
                        for s2 in range(2):
                            for kc in range(8):
                                mm(bank(s2), scbc[:, s2, kc, :], buf[:, kc, :], kc == 0, kc == 7, [key, "scbc"], [s2])
                            P.op("dve", lambda e, l=l, half=half, s2=s2: e.tensor_tensor(
                                out=gaterow[s2][l][:, half * 512:(half + 1) * 512], in0=bank(s2), in1=abrow, op=ALU.add),
                                reads=["abrow"] + bk(s2), writes=["gaterow%d_%d" % (s2, l)], excl=bk(s2))
            P.op("dve", lambda e: e.tensor_tensor(out=modc[:, :], in0=PM[:, 0:64], in1=abcol[:, :], op=ALU.add),
                 reads=["abcol"] + bk(7), writes=["modc"], excl=bk(7))
            for l in range(2):
                P.dma(lambda e, l=l: e.dma_start(out=gsc_d[l], in_=gaterow[1][l]), reads=["gaterow1_%d" % l], writes=["gsc"], chan="gsc%d" % l)

        def slot_prep(s):
            P.fence()
            st_adaw, st_w, c0f, hi32, scbc, gaterow_all, abrow, psrow = staging()
            gaterow = gaterow_all[s]
            if s == 1:
                for l in range(2):
                    P.dma(lambda e, l=l: e.dma_start(out=gaterow[l], in_=gsc_d[l]), reads=["gsc"], writes=["gaterow1_%d" % l], chan="gsc%d" % l)
            m3 = modc[:, :].rearrange("p (c s) -> p c s", s=2)
            for l in range(2):
                P.op("dve", lambda e, l=l: e.scalar_tensor_tensor(
                    out=Acol[:, l * 8:(l + 1) * 8], in0=m3[:, l * 16 + 8:l * 16 + 16, s], scalar=1.0,
                    in1=gcol[:, l * 8:(l + 1) * 8], op0=ALU.add, op1=ALU.mult),
                    reads=["modc", "gcol"], writes=["Acol"])

            def shcol(l, kc):
                c = (l * 16 + kc) * 2 + s
                return modc[:, c:c + 1]

            nw = [0]

            def stage_rows(src_ap, ncols):
                i = nw[0] % len(st_w)
                nw[0] += 1
                buf = st_w[i][:, 0:ncols]
                key = "st_w%d" % i
                P.dma(lambda e: e.dma_start(out=buf, in_=src_ap), writes=[key], chan=key)
                return buf, key

            nw = [0]

            def stage_rows(src_ap, ncols):
                i = nw[0] % len(st_w)
                nw[0] += 1
                buf = st_w[i][:, 0:ncols]
                key = "st_w%d" % i
                P.dma(lambda e: e.dma_start(out=buf, in_=src_ap), writes=[key], chan=key)
                return buf, key

            for kc in range(8):
                buf, key = stage_rows(win0_d[kc * 128:(kc + 1) * 128, :], 2560)
                P.op("dve", lambda e, buf=buf, kc=kc: e.tensor_scalar(
                    out=W0qg[:, kc, 0:1024], in0=buf[:, 0:1024], scalar1=Acol[:, kc:kc + 1], scalar2=None, op0=ALU.mult),
                    reads=[key, "Acol"], writes=["W0qg"])
                P.op("dve", lambda e, buf=buf, kc=kc: e.tensor_scalar(
                    out=W0kv[:, kc, :], in0=buf[:, 1024:1536], scalar1=Acol[:, kc:kc + 1], scalar2=None, op0=ALU.mult),
                    reads=[key, "Acol"], writes=["W0kv"])
                P.op("dve", lambda e, buf=buf, kc=kc: e.tensor_scalar(
                    out=W0qg[:, kc, 1024:2048], in0=buf[:, 1536:2560], scalar1=Acol[:, kc:kc + 1], scalar2=None, op0=ALU.mult),
                    reads=[key, "Acol"], writes=["W0qg"])
                for cb in range(5):
                    mm(bank(cb)[0:1, :], shcol(0, kc), buf[:, cb * 512:(cb + 1) * 512], kc == 0, kc == 7,
                       [key, "modc"], [cb])
            for cb in range(5):
                P.op("dve", lambda e, cb=cb: e.tensor_copy(out=c0f[0:1, cb * 512:(cb + 1) * 512], in_=bank(cb)[0:1, :]),
                     reads=bk(cb), writes=["c0f"], excl=bk(cb))
            P.op("dve", lambda e: e.tensor_copy(out=c0hl[0:1, :], in_=c0f[0:1, :]), reads=["c0f"], writes=["c0hl"])
            P.op("dve", lambda e: e.tensor_copy(out=hi32[0:1, :], in_=c0hl[0:1, :]), reads=["c0hl"], writes=["hi32"])
            P.op("dve", lambda e: e.tensor_tensor(out=c0hl[32:33, :], in0=c0f[0:1, :], in1=hi32[0:1, :], op=ALU.subtract),
                 reads=["c0f", "hi32"], writes=["c0hl"])

            for kc in range(8):
                buf, key = stage_rows(wout0_d[kc * 128:(kc + 1) * 128, :], 1024)
                P.op("dve", lambda e, buf=buf, kc=kc: e.scalar_tensor_tensor(
                    out=Wo0[:, kc, :], in0=buf, scalar=0.5, in1=gaterow[0], op0=ALU.mult, op1=ALU.mult),
                    reads=[key, "gaterow%d_0" % s], writes=["Wo0"])

            for kc in range(8):
                buf, key = stage_rows(win1_d[kc * 128:(kc + 1) * 128, :], 2048)
                P.op("dve", lambda e, buf=buf, kc=kc: e.tensor_scalar(
                    out=W1[:, kc, :], in0=buf, scalar1=Acol[:, 8 + kc:8 + kc + 1], scalar2=None, op0=ALU.mult),
                    reads=[key, "Acol"], writes=["W1"])
                for j in range(16):
                    mm(PM[:, 64 + j:64 + j + 1], buf[:, j * 128:(j + 1) * 128], shcol(1, kc),
                       kc == 0 and j == 0, kc == 7 and j == 15, [key, "modc"], [7], skip=True)
            P.op("dve", lambda e: e.tensor_copy(out=c1col[:, :], in_=PM[:, 64:80]), reads=bk(7), writes=["c1col"], excl=bk(7))

            P.dma(lambda e: e.dma_start(out=psrow, in_=pscale_d[0:1, :].partition_broadcast(128)), writes=["psrow"], chan="psrow")
            for kc in range(8):
                buf, key = stage_rows(wgrp_d[kc * 128:(kc + 1) * 128, :], 256)
                g = kc // 2
                P.op("dve", lambda e, buf=buf, kc=kc, g=g: e.tensor_tensor(
                    out=Wg[:, kc, :], in0=buf, in1=psrow[:, g * 256:(g + 1) * 256], op=ALU.mult),
                    reads=[key, "psrow"], writes=["Wg"])
            for kc in range(8):
                buf, key = stage_rows(wout1_d[kc * 128:(kc + 1) * 128, :], 1024)
                P.op("dve", lambda e, buf=buf, kc=kc: e.scalar_tensor_tensor(
                    out=Wo1[:, kc, :], in0=buf, scalar=0.5, in1=gaterow[1], op0=ALU.mult, op1=ALU.mult),
                    reads=[key, "gaterow%d_1" % s], writes=["Wo1"])
            P.fence()

        SM = {"ssx": 8, "vx": 9, "rstdx": 10}

        def rms_scale(src, skey, dst, dkey, junk, jkey, eng_scale, sc0):
            P.op("dve", lambda e: e.scalar_tensor_tensor(out=junk, in0=src, scalar=1.0, in1=src, op0=ALU.mult, op1=ALU.mult,
                                                         accum_out=small[:, sc0:sc0 + 1]),
                 reads=[skey], writes=[jkey, "sm%d" % sc0])
            P.op("pool", lambda e: e.tensor_scalar(out=small[:, sc0 + 1:sc0 + 2], in0=small[:, sc0:sc0 + 1], scalar1=1.0 / D,
                                                   scalar2=EPS, op0=ALU.mult, op1=ALU.add), reads=["sm%d" % sc0], writes=["sm%d" % (sc0 + 1)])
            P.op("pool", lambda e: e.tensor_tensor(out=small[:, sc0 + 2:sc0 + 3], in0=small[:, sc0 + 1:sc0 + 2], in1=mhalf[:, :], op=ALU.pow),
                 reads=["sm%d" % (sc0 + 1), "mhalf"], writes=["sm%d" % (sc0 + 2)])
            if eng_scale == "act":
                P.op("act", lambda e: e.activation(out=dst, in_=src, func=AF.Copy, scale=small[:, sc0 + 2:sc0 + 3]),
                     reads=[skey, "sm%d" % (sc0 + 2)], writes=[dkey])
            else:
                P.op("dve", lambda e: e.tensor_scalar(out=dst, in0=src, scalar1=small[:, sc0 + 2:sc0 + 3], scalar2=None, op0=ALU.mult),
                     reads=[skey, "sm%d" % (sc0 + 2)], writes=[dkey])

        def tr8(src, skey, n, bnk):
            tp = bank(bnk).bitcast(BF16)
            for i in range(n):
                P.op("pe", lambda e, i=i: e.transpose(out=tp[:, i * 128:(i + 1) * 128], in_=src[:, i * 128:(i + 1) * 128],
                                                      identity=ident[:, :]),
                     reads=[skey, "ident"], writes=bk(bnk), excl=bk(bnk))
            return tp

        def rope_tables(src_ap, which, cs_, ckey, tq_, tkey):
            P.dma(lambda e: e.dma_start(out=cs_, in_=src_ap), writes=[ckey], chan=ckey)
            g2 = gtab[:, which, :].rearrange("p (i two) -> p i two", two=2)
            ge, go = g2[:, :, 0], g2[:, :, 1]
            for i, (a, b_) in enumerate([(cs_[:, 0:64], ge), (cs_[:, 64:128], go), (cs_[:, 64:128], ge), (cs_[:, 0:64], go)]):
                P.op("pool", lambda e, i=i, a=a, b_=b_: e.tensor_tensor(out=tq_[:, i, :], in0=a, in1=b_, op=ALU.mult),
                     reads=[ckey, "gtab"], writes=[tkey])

        def norm_rope(raw, rkey, sq, sqkey, nh, sc0, tq_, tkey, dst_bf, dkey):
            n = nh * 128
            for h in range(nh):
                P.op("dve", lambda e, h=h: e.scalar_tensor_tensor(
                    out=sq.bitcast(BF16)[:, 0:128], in0=raw[:, h * 128:(h + 1) * 128], scalar=1.0, in1=raw[:, h * 128:(h + 1) * 128],
                    op0=ALU.mult, op1=ALU.mult, accum_out=small[:, sc0 + h:sc0 + h + 1]), reads=[rkey], writes=[sqkey, "q%d" % sc0])
            P.op("pool", lambda e: e.tensor_scalar(out=small[:, sc0 + 4:sc0 + 4 + nh], in0=small[:, sc0:sc0 + nh], scalar1=1.0 / HD,
                                                   scalar2=EPS, op0=ALU.mult, op1=ALU.add), reads=["q%d" % sc0], writes=["q%d" % (sc0 + 4)])
            P.op("pool", lambda e: e.tensor_tensor(out=small[:, sc0 + 8:sc0 + 8 + nh], in0=small[:, sc0 + 4:sc0 + 4 + nh],
                                                   in1=mhalf[:, 0:1].to_broadcast([128, nh]), op=ALU.pow),
                 reads=["q%d" % (sc0 + 4), "mhalf"], writes=["q%d" % (sc0 + 8)])
            r3 = raw.rearrange("p (h d) -> p h d", h=nh)
            P.op("dve", lambda e: e.tensor_tensor(out=r3, in0=r3, in1=small[:, sc0 + 8:sc0 + 8 + nh].unsqueeze(2).to_broadcast([128, nh, HD]),
                                                  op=ALU.mult), reads=[rkey, "q%d" % (sc0 + 8)], writes=[rkey])
            q4 = raw.rearrange("p (h i two) -> p h i two", h=nh, two=2)
            r4 = dst_bf.rearrange("p (h i two) -> p h i two", h=nh, two=2)
            t1 = sq[:, 0:nh * 64].rearrange("p (h i) -> p h i", h=nh)
            t2 = sq[:, nh * 64:nh * 128].rearrange("p (h i) -> p h i", h=nh)

            def tb(i):
                return tq_[:, i, :].unsqueeze(1).to_broadcast([128, nh, 64])
            P.op("dve", lambda e: e.tensor_tensor(out=t1, in0=q4[:, :, :, 0], in1=tb(0), op=ALU.mult), reads=[rkey, tkey], writes=[sqkey])
            P.op("dve", lambda e: e.tensor_tensor(out=t2, in0=q4[:, :, :, 1], in1=tb(1), op=ALU.mult), reads=[rkey, tkey], writes=[sqkey])
            P.op("dve", lambda e: e.tensor_tensor(out=r4[:, :, :, 0], in0=t1, in1=t2, op=ALU.subtract), reads=[sqkey], writes=[dkey])
            P.op("dve", lambda e: e.tensor_tensor(out=t1, in0=q4[:, :, :, 0], in1=tb(2), op=ALU.mult), reads=[rkey, tkey], writes=[sqkey])
            P.op("dve", lambda e: e.tensor_tensor(out=t2, in0=q4[:, :, :, 1], in1=tb(3), op=ALU.mult), reads=[rkey, tkey], writes=[sqkey])
            P.op("dve", lambda e: e.tensor_tensor(out=r4[:, :, :, 1], in0=t1, in1=t2, op=ALU.add), reads=[sqkey], writes=[dkey])

        def step_all(gens):
            for gs in list(gens):
                if gs[1] > 0:
                    gs[1] -= 1
                    continue
                try:
                    n = next(gs[0])
                    gs[1] = (n or 1) - 1
                except StopIteration:
                    gens.remove(gs)

        def drain(gens):
            while gens:
                step_all(gens)

        def rope_apply(raw, rkey, nh, tq_, tkey, tmp, tmpkey, dst_bf, dkey):
            q4 = raw.rearrange("p (h i two) -> p h i two", h=nh, two=2)
            r4 = dst_bf.rearrange("p (h i two) -> p h i two", h=nh, two=2)
            t1 = tmp[:, 0:nh * 64].rearrange("p (h i) -> p h i", h=nh)
            t2 = tmp[:, nh * 64:nh * 128].rearrange("p (h i) -> p h i", h=nh)

            def tb(i):
                return tq_[:, i, :].unsqueeze(1).to_broadcast([128, nh, 64])
            P.op("dve", lambda e: e.tensor_tensor(out=t1, in0=q4[:, :, :, 0], in1=tb(0), op=ALU.mult), reads=[rkey, tkey], writes=[tmpkey])
            P.op("dve", lambda e: e.tensor_tensor(out=t2, in0=q4[:, :, :, 1], in1=tb(1), op=ALU.mult), reads=[rkey, tkey], writes=[tmpkey])
            P.op("dve", lambda e: e.tensor_tensor(out=r4[:, :, :, 0], in0=t1, in1=t2, op=ALU.subtract), reads=[tmpkey], writes=[dkey])
            P.op("dve", lambda e: e.tensor_tensor(out=t1, in0=q4[:, :, :, 0], in1=tb(2), op=ALU.mult), reads=[rkey, tkey], writes=[tmpkey])
            P.op("dve", lambda e: e.tensor_tensor(out=t2, in0=q4[:, :, :, 1], in1=tb(3), op=ALU.mult), reads=[rkey, tkey], writes=[tmpkey])
            P.op("dve", lambda e: e.tensor_tensor(out=r4[:, :, :, 1], in0=t1, in1=t2, op=ALU.add), reads=[tmpkey], writes=[dkey])

        def kv_tile(s, t):
            x_, xkey = kv_x[t % 4], "kvx%d" % (t % 4)
            cs_, ckey = kv_cs[t % 3], "kvcs%d" % (t % 3)
            tq_, tkey = kv_tq[t % 2], "kvtq%d" % (t % 2)
            xn_, nkey = kv_xn[t % 2], "kvxn%d" % (t % 2)
            xnT_, tkey2 = kv_xnT[t % 2], "kvxnT%d" % (t % 2)
            kr_, rkey = kv_krot[t % 2], "kvkr%d" % (t % 2)
            mb = t % 4
            tb_ = 4 + t % 2
            kb = 6 + t % 2
            c0 = 4 * (t % 4)
            k0 = 8 * (t % 4)
            P.dma(lambda e: e.dma_start(out=x_, in_=xs_d[s][t * 128:(t + 1) * 128, :]), writes=[xkey], chan=xkey)
            yield 2
            P.op("act", lambda e: e.activation(out=kv_junk, in_=x_, func=AF.Square, accum_out=small[:, 48 + c0:49 + c0]),
                 reads=[xkey], writes=["kvjunk", "kvs%d" % c0])
            yield 1
            P.op("pool", lambda e: e.tensor_scalar(out=small[:, 49 + c0:50 + c0], in0=small[:, 48 + c0:49 + c0], scalar1=1.0 / D, scalar2=EPS,
                                                   op0=ALU.mult, op1=ALU.add), reads=["kvs%d" % c0], writes=["kvv%d" % c0])
            P.op("pool", lambda e: e.tensor_tensor(out=small[:, 50 + c0:51 + c0], in0=small[:, 49 + c0:50 + c0], in1=mhalf[:, :], op=ALU.pow),
                 reads=["kvv%d" % c0, "mhalf"], writes=["kvr%d" % c0])
            yield 1
            P.op("act", lambda e: e.activation(out=xn_, in_=x_, func=AF.Copy, scale=small[:, 50 + c0:51 + c0]),
                 reads=[xkey, "kvr%d" % c0], writes=[nkey])
            yield 1
            tp = tr8(xn_, nkey, 8, tb_)
            yield 1
            P.op("dve", lambda e: e.tensor_copy(out=xnT_.rearrange("p k t -> p (k t)"), in_=tp), reads=bk(tb_), writes=[tkey2], excl=bk(tb_))
            P.dma(lambda e: e.dma_start(out=cs_, in_=rope_d[t * 128:(t + 1) * 128, :]), writes=[ckey], chan=ckey)
            yield 1
            for kc in range(8):
                mm(bank(mb), xnT_[:, kc, :], W0kv[:, kc, :], kc == 0, False, [tkey2, "W0kv"], [mb])
            mm(bank(mb), onesel[:, :], c0hl[:, 1024:1536], False, True, ["onesel", "c0hl"], [mb])
            yield 1
            P.op("act", lambda e: e.activation(out=VA[:, t, :, 0:128], in_=bank(mb)[:, 256:512].rearrange("p (h d) -> p h d", h=2),
                                               func=AF.Copy), reads=bk(mb), writes=[("VA", t)], excl=bk(mb))
            for h in range(2):
                P.op("act", lambda e, h=h: e.activation(out=kv_junkA[:, h * 128:(h + 1) * 128], in_=bank(mb)[:, h * 128:(h + 1) * 128],
                                                        func=AF.Square, accum_out=small[:, k0 + h:k0 + h + 1]),
                     reads=bk(mb), writes=["kvjunkA", "kvks%d" % k0], excl=bk(mb))
            yield 1
            P.op("pool", lambda e: e.tensor_scalar(out=small[:, k0 + 2:k0 + 4], in0=small[:, k0:k0 + 2], scalar1=1.0 / HD, scalar2=EPS,
                                                   op0=ALU.mult, op1=ALU.add), reads=["kvks%d" % k0], writes=["kvkv%d" % k0])
            P.op("pool", lambda e: e.tensor_tensor(out=small[:, k0 + 4:k0 + 6], in0=small[:, k0 + 2:k0 + 4],
                                                   in1=mhalf[:, 0:1].to_broadcast([128, 2]), op=ALU.pow),
                 reads=["kvkv%d" % k0, "mhalf"], writes=["kvkr%d" % k0])
            g2 = gtab[:, 1, :].rearrange("p (i two) -> p i two", two=2)
            ge, go = g2[:, :, 0], g2[:, :, 1]
            for i, (a, b_) in enumerate([(cs_[:, 0:64], ge), (cs_[:, 64:128], go), (cs_[:, 64:128], ge), (cs_[:, 0:64], go)]):
                P.op("pool", lambda e, i=i, a=a, b_=b_: e.tensor_tensor(out=tq_[:, i, :], in0=a, in1=b_, op=ALU.mult),
                     reads=[ckey, "gtab"], writes=[tkey])
            yield 1
            P.op("dve", lambda e: e.tensor_tensor(out=kv_kn.rearrange("p (h d) -> p h d", h=2),
                                                  in0=bank(mb)[:, 0:256].rearrange("p (h d) -> p h d", h=2),
                                                  in1=small[:, k0 + 4:k0 + 6].unsqueeze(2).to_broadcast([128, 2, HD]), op=ALU.mult),
                 reads=["kvkr%d" % k0] + bk(mb), writes=["kvkn"], excl=bk(mb))
            rope_apply(kv_kn, "kvkn", 2, tq_, tkey, kv_tmp, "kvtmp", kr_, rkey)
            yield 1
            tp2 = tr8(kr_, rkey, 2, kb)
            yield 1
            P.op("dve", lambda e: e.tensor_copy(out=KT[:, :, t * 128:(t + 1) * 128], in_=tp2[:, 0:256].rearrange("p (h t) -> p h t", h=2)),
                 reads=bk(kb), writes=[("KT", t)], excl=bk(kb))

        def kv_phase(s):
            P.op("pool", lambda e: e.memset(VA[:, :, :, 128:129], 1.0), writes=["VAones"])
            gens = []
            t = 0
            while t < NT or gens:
                if t < NT:
                    gens.append([kv_tile(s, t), 0])
                    t += 1
                step_all(gens)

        def o_ap(hh, n, g):
            ob = 4 if g == 0 else 6
            return bank(ob)[:, hh * 129:hh * 129 + n] if hh < 3 else bank(ob + 1)[:, 0:n]

        def o_bank(hh, g):
            ob = 4 if g == 0 else 6
            return ob if hh < 3 else ob + 1

        def front_gen(blk):
            k, par = blk["k"], blk["k"] % 2
            xi = (k - 1) % 3
            xk = "xt%d" % xi
            b6, b7 = side_pair[0]
            tp = None
            rms_scale(xt[xi], xk, tokbf, "tokbf", scrA.bitcast(BF16)[:, 0:D], "scrA", "dve", 8)
            yield 1
            rope_tables(blk["rope"], 0, cs, "cs", tq, "tq")
            yield 5
            b6, b7 = side_pair[0]
            tp = tr8(tokbf, "tokbf", 8, b7)
            yield 1
            P.op("dve", lambda e, b6=b6, b7=b7, tp=tp: e.tensor_copy(out=xnT.rearrange("p k t -> p (k t)"), in_=tp), reads=bk(b7), writes=["xnT"], excl=bk(b7))
            yield 1
            for half in range(2):
                b6, b7 = side_pair[0]
                for kc in range(8):
                    mm(bank(b6), xnT[:, kc, :], W0qg[:, kc, half * 512:(half + 1) * 512], kc == 0, False, ["xnT", "W0qg"], [b6])
                mm(bank(b6), onesel[:, :], c0hl[:, half * 512:(half + 1) * 512], False, True, ["onesel", "c0hl"], [b6])
                yield 1
                P.op("dve", lambda e, b6=b6, b7=b7, tp=tp: e.tensor_copy(out=scrB, in_=bank(b6)), reads=bk(b6), writes=["scrB"], excl=bk(b6))
                norm_rope(scrB, "scrB", scrA, "scrA", 4, 16, tq, "tq", tokbf[:, half * 512:(half + 1) * 512], "tokbf")
                yield 2
            yield 6
            b6, b7 = side_pair[0]
            tp = tr8(tokbf, "tokbf", 8, b7)
            yield 1
            P.op("dve", lambda e, b6=b6, b7=b7, tp=tp: e.tensor_copy(out=QT[par].rearrange("p h t -> p (h t)"), in_=tp), reads=bk(b7), writes=["QT%d" % par],
                 excl=bk(b7))
            for half in range(2):
                b6, b7 = side_pair[0]
                for kc in range(8):
                    mm(bank(b6), xnT[:, kc, :], W0qg[:, kc, 1024 + half * 512:1024 + (half + 1) * 512], kc == 0, False, ["xnT", "W0qg"], [b6])
                mm(bank(b6), onesel[:, :], c0hl[:, 1536 + half * 512:1536 + (half + 1) * 512], False, True, ["onesel", "c0hl"], [b6])
                yield 2
                P.op("act", lambda e, b6=b6, b7=b7, tp=tp: e.activation(out=thb, in_=bank(b6), func=AF.Tanh, scale=0.5), reads=bk(b6), writes=["thb"], excl=bk(b6))
                P.op("dve", lambda e, half=half, b6=b6, b7=b7, tp=tp: e.scalar_tensor_tensor(out=sg[par][:, half * 512:(half + 1) * 512], in0=thb, scalar=1.0,
                                                                        in1=bank(b6), op0=ALU.add, op1=ALU.mult),
                     reads=["thb"] + bk(b6), writes=["sg%d" % par], excl=bk(b6))
                yield 1

        side_pair = [(6, 7)]

        def finalize(blk, g):
            par = blk["k"] % 2
            ob = 4 if g == 0 else 6
            rs0 = 40 + 8 * par + 4 * g
            P.op("dve", lambda e, ob=ob: e.tensor_copy(out=small[:, rs0:rs0 + 3],
                                                in_=bank(ob)[:, 0:387].rearrange("p (h d) -> p h d", d=129)[:, :, 128]),
                 reads=bk(ob), writes=["rs%d" % par], excl=bk(ob))
            P.op("dve", lambda e, ob=ob: e.tensor_copy(out=small[:, rs0 + 3:rs0 + 4], in_=bank(ob + 1)[:, 128:129]), reads=bk(ob + 1), writes=["rs%d" % par],
                 excl=bk(ob + 1))
            h0 = 4 * g
            P.op("dve", lambda e, ob=ob: e.tensor_tensor(
                out=AO[:, h0 * 128:(h0 + 3) * 128].rearrange("p (h d) -> p h d", h=3),
                in0=bank(ob)[:, 0:387].rearrange("p (h d) -> p h d", d=129)[:, :, 0:128],
                in1=sg[par][:, h0 * 128:(h0 + 3) * 128].rearrange("p (h d) -> p h d", h=3), op=ALU.mult),
                reads=["sg%d" % par] + bk(ob), writes=["AO"], excl=bk(ob))
            P.op("dve", lambda e, ob=ob: e.tensor_tensor(out=AO[:, (h0 + 3) * 128:(h0 + 4) * 128], in0=bank(ob + 1)[:, 0:128],
                                                  in1=sg[par][:, (h0 + 3) * 128:(h0 + 4) * 128], op=ALU.mult),
                 reads=["sg%d" % par] + bk(ob + 1), writes=["AO"], excl=bk(ob + 1))

        def attention(blk, gens):
            par = blk["k"] % 2
            total = NKV * NPAIR
            started = set()

            def qk(idx, t):
                g, j = divmod(idx, NPAIR)
                b = idx % 2
                st = 2 * j + t
                mm(bank(2 * b + t), KT[:, g, st * 128:(st + 1) * 128],
                   QT[par][:, 4 * g:4 * g + 4, :].rearrange("p h t -> p (h t)"), True, True, ["QT%d" % par, ("KT", st)], [2 * b + t])

            def ex(idx):
                b = idx % 2
                for t in range(2):
                    P.op("act", lambda e, t=t: e.activation(out=PT[b][:, t * 512:(t + 1) * 512], in_=bank(2 * b + t), func=AF.Exp,
                                                            bias=negM[:, 0:1], scale=1.0),
                         reads=["negM"] + bk(2 * b + t), writes=["PT%d_%d" % (b, t)], excl=bk(2 * b + t))

            def pv(idx, t):
                g, j = divmod(idx, NPAIR)
                b = idx % 2
                st = 2 * j + t
                for hh in range(4):
                    bnk = o_bank(hh, g)
                    first = (g, bnk) not in started
                    started.add((g, bnk))
                    mm(o_ap(hh, 129, g), PT[b][:, t * 512 + hh * 128:t * 512 + (hh + 1) * 128], VA[:, st, g, 0:129],
                       first, False, ["PT%d_%d" % (b, t), ("VA", st), "VAones"], [bnk], skip=True)

            qk(0, 0)
            qk(0, 1)
            for idx in range(total):
                ex(idx)
                for t in range(2):
                    if idx + 1 < total:
                        qk(idx + 1, t)
                    pv(idx, t)
                if idx % NPAIR == NPAIR - 1:
                    finalize(blk, idx // NPAIR)
                side_pair[0] = (6, 7) if ((idx + 1) // NPAIR) % NKV == 0 else (4, 5)
                step_all(gens)

        def back_gen(blk, prev):
            k = blk["k"]
            xi = k % 3
            xk = "xt%d" % xi
            ub = blk["ub"]
            uk = "UT%d" % ub
            b6, b7 = side_pair[0]
            tp = None
            if NPAIR >= 8:
                yield 1
            par = k % 2
            P.op("dve", lambda e, b6=b6, b7=b7, tp=tp: e.reciprocal(out=small[:, 56:64], in_=small[:, 40 + 8 * par:48 + 8 * par]), reads=["rs%d" % par], writes=["rec"])
            for h in range(8):
                P.op("dve", lambda e, h=h, b6=b6, b7=b7, tp=tp: e.tensor_scalar(out=AO[:, h * 128:(h + 1) * 128], in0=AO[:, h * 128:(h + 1) * 128],
                                                           scalar1=small[:, 56 + h:57 + h], scalar2=None, op0=ALU.mult),
                     reads=["rec", "AO"], writes=["AO"])
            if NPAIR >= 8:
                yield 2
            b6, b7 = side_pair[0]
            tp = tr8(AO, "AO", 8, b7)
            yield 1
            P.op("dve", lambda e, b6=b6, b7=b7, tp=tp: e.tensor_copy(out=xnT.rearrange("p k t -> p (k t)"), in_=tp), reads=bk(b7), writes=["xnT"], excl=bk(b7))
            yield 1
            for half in range(2):
                b6, b7 = side_pair[0]
                for hc in range(8):
                    mm(bank(b6), xnT[:, hc, :], Wo0[:, hc, half * 512:(half + 1) * 512], hc == 0, hc == 7, ["xnT", "Wo0"], [b6])
                yield 1
                P.op("dve", lambda e, half=half, b6=b6, b7=b7, tp=tp: e.tensor_tensor(out=xt[xi][:, half * 512:(half + 1) * 512],
                                                                 in0=xt[xi][:, half * 512:(half + 1) * 512], in1=bank(b6), op=ALU.add),
                     reads=[xk] + bk(b6), writes=[xk], excl=bk(b6))
            rms_scale(xt[xi], xk, tokbf, "tokbf", scrA.bitcast(BF16)[:, 0:D], "scrA", "dve", 8)
            yield 6
            b6, b7 = side_pair[0]
            tp = tr8(tokbf, "tokbf", 8, b7)
            yield 1
            P.op("dve", lambda e, b6=b6, b7=b7, tp=tp: e.tensor_copy(out=xnT.rearrange("p k t -> p (k t)"), in_=tp), reads=bk(b7), writes=["xnT"], excl=bk(b7))
            yield 1
            for half in range(2):
                b6, b7 = side_pair[0]
                for j4 in range(4):
                    j = half * 4 + j4
                    for kc in range(8):
                        mm(bank(b6)[:, j4 * 128:(j4 + 1) * 128], W1[:, kc, j * 128:(j + 1) * 128], xnT[:, kc, :], kc == 0, kc == 7,
                           ["xnT", "W1"], [b6])
                yield 1
                if blk["halo"]:
                    P.op("dve", lambda e, half=half, b6=b6, b7=b7, tp=tp: e.tensor_tensor(
                        out=UH[:, half * 4:(half + 1) * 4, :], in0=bank(b6).rearrange("p (k t) -> p k t", k=4)[:, :, 0:16],
                        in1=c1col[:, half * 4:(half + 1) * 4].unsqueeze(2).to_broadcast([128, 4, 16]), op=ALU.add),
                        reads=["c1col"] + bk(b6), writes=["UH"], excl=bk(b6))
                else:
                    P.op("dve", lambda e, half=half, b6=b6, b7=b7, tp=tp: e.tensor_tensor(
                        out=UT[ub][:, half * 4:(half + 1) * 4, 8:136], in0=bank(b6).rearrange("p (k t) -> p k t", k=4),
                        in1=c1col[:, half * 4:(half + 1) * 4].unsqueeze(2).to_broadcast([128, 4, 128]), op=ALU.add),
                        reads=["c1col"] + bk(b6), writes=[uk], excl=bk(b6))
            if blk["halo"]:
                P.op("dve", lambda e, b6=b6, b7=b7, tp=tp: e.tensor_tensor(out=UH[:, :, :], in0=UH[:, :, :],
                                                      in1=maskh[:, :].unsqueeze(1).to_broadcast([128, 8, 16]), op=ALU.mult),
                     reads=["UH", "maskh"], writes=["UH"])
                return
            for half in range(2):
                b6, b7 = side_pair[0]
                for j4 in range(4):
                    j = half * 4 + j4
                    for kc in range(8):
                        mm(bank(b7)[:, j4 * 128:(j4 + 1) * 128], W1[:, kc, D + j * 128:D + (j + 1) * 128], xnT[:, kc, :], kc == 0, kc == 7,
                           ["xnT", "W1"], [b7])
                yield 1
                g3 = scrB.rearrange("p (k t) -> p k t", k=4)
                P.op("dve", lambda e, half=half, b6=b6, b7=b7, tp=tp: e.tensor_tensor(
                    out=g3, in0=bank(b7).rearrange("p (k t) -> p k t", k=4),
                    in1=c1col[:, 8 + half * 4:8 + (half + 1) * 4].unsqueeze(2).to_broadcast([128, 4, 128]), op=ALU.add),
                    reads=["c1col"] + bk(b7), writes=["scrB"], excl=bk(b7))
                yield 2
                P.op("act", lambda e, b6=b6, b7=b7, tp=tp: e.activation(out=scrA[:, 0:512], in_=scrB, func=AF.Tanh, scale=0.5), reads=["scrB"], writes=["scrA"])
                P.op("dve", lambda e, half=half, b6=b6, b7=b7, tp=tp: e.scalar_tensor_tensor(
                    out=sg1T[ub][:, half * 4:(half + 1) * 4, :].rearrange("p k t -> p (k t)"), in0=scrA[:, 0:512], scalar=1.0, in1=scrB,
                    op0=ALU.add, op1=ALU.mult), reads=["scrA", "scrB"], writes=["sg1T%d" % ub])
                yield 1
            if blk["first"]:
                if blk["slot"] == 0:
                    P.op("pool", lambda e, b6=b6, b7=b7, tp=tp: e.memset(UT[ub][:, :, 0:8], 0.0), writes=[uk])
                else:
                    P.op("pool", lambda e, b6=b6, b7=b7, tp=tp: e.tensor_copy(out=UT[ub][:, :, 0:8], in_=UH[:, :, 0:8]), reads=["UH"], writes=[uk])
            else:
                pk = "UT%d" % (1 - ub)
                P.op("pool", lambda e, b6=b6, b7=b7, tp=tp: e.tensor_copy(out=UT[ub][:, :, 0:8], in_=UT[1 - ub][:, :, 128:136]), reads=[pk], writes=[uk])
                P.op("pool", lambda e, b6=b6, b7=b7, tp=tp: e.tensor_copy(out=UT[1 - ub][:, :, 136:144], in_=UT[ub][:, :, 8:16]), reads=[uk], writes=[pk])
            if prev is not None and not prev["halo"]:
                yield 1
                yield from l1back_gen(prev)

        def l1back_gen(pb):
            k = pb["k"]
            xi = k % 3
            xk = "xt%d" % xi
            ub = pb["ub"]
            uk = "UT%d" % ub
            U = UT[ub]
            b6, b7 = side_pair[0]
            tp = None
            TA = scrA[:, 0:288].rearrange("p (c n) -> p c n", c=2)
            TB = scrA[:, 288:576].rearrange("p (c n) -> p c n", c=2)
            mixT = tokbf.rearrange("p (k t) -> p k t", k=8)
            ci = pb["corr"]
            if ci is not None:
                corr = scrB.rearrange("p (g t) -> p g t", g=4)
                P.dma(lambda e, b6=b6, b7=b7, tp=tp: e.dma_start(out=scrB, in_=corr_d[0:1, ci * 512:(ci + 1) * 512].partition_broadcast(128)),
                      writes=["scrB"], chan="corr")

            def add(out, a, b_, rk):
                P.op("dve", lambda e, b6=b6, b7=b7, tp=tp: e.tensor_tensor(out=out, in0=a, in1=b_, op=ALU.add), reads=rk + ["scrA"], writes=["scrA"])

            for g in range(4):
                c0, c1 = 2 * g, 2 * g + 2
                w = WINS[g]
                if g == 0:
                    add(TB[:, :, 0:128], U[:, c0:c1, 7:135], U[:, c0:c1, 8:136], [uk])
                elif g == 1:
                    add(TA[:, :, 0:130], U[:, c0:c1, 6:136], U[:, c0:c1, 7:137], [uk])
                    add(TB[:, :, 0:128], TA[:, :, 0:128], TA[:, :, 2:130], [])
                elif g == 2:
                    add(TA[:, :, 0:135], U[:, c0:c1, 4:139], U[:, c0:c1, 5:140], [uk])
                    add(TB[:, :, 0:132], TA[:, :, 0:132], TA[:, :, 2:134], [])
                    add(TA[:, :, 0:128], TB[:, :, 0:128], TB[:, :, 4:132], [])
                else:
                    add(TA[:, :, 0:142], U[:, c0:c1, 0:142], U[:, c0:c1, 1:143], [uk])
                    add(TB[:, :, 0:140], TA[:, :, 0:140], TA[:, :, 2:142], [])
                    add(TA[:, :, 0:136], TB[:, :, 0:136], TB[:, :, 4:140], [])
                    add(TB[:, :, 0:128], TA[:, :, 0:128], TA[:, :, 8:136], [])
                res = TA if g == 2 else TB
                if ci is not None:
                    P.op("dve", lambda e, res=res, g=g, b6=b6, b7=b7, tp=tp: e.tensor_tensor(
                        out=res[:, :, 0:128], in0=res[:, :, 0:128],
                        in1=corr[:, g, :].unsqueeze(1).to_broadcast([128, 2, 128]), op=ALU.mult),
                        reads=["scrA", "scrB"], writes=["scrA"])
                P.op("dve", lambda e, res=res, c0=c0, c1=c1, w=w, b6=b6, b7=b7, tp=tp: e.scalar_tensor_tensor(
                    out=mixT[:, c0:c1, :], in0=res[:, :, 0:128], scalar=1.0 / w, in1=U[:, c0:c1, 8:136],
                    op0=ALU.mult, op1=ALU.subtract), reads=["scrA", uk], writes=["tokbf"])
                if g % 2 == 1:
                    yield 1
            yield 5
            for half in range(2):
                b6, b7 = side_pair[0]
                for gg in range(2):
                    g = half * 2 + gg
                    for dc in range(2):
                        j4 = gg * 2 + dc
                        for cc in range(2):
                            mm(bank(b6)[:, j4 * 128:(j4 + 1) * 128], Wg[:, g * 2 + cc, dc * 128:(dc + 1) * 128],
                               mixT[:, g * 2 + cc, :], cc == 0, cc == 1, ["tokbf", "Wg"], [b6])
                yield 1
                P.op("dve", lambda e, half=half, b6=b6, b7=b7, tp=tp: e.tensor_tensor(
                    out=xnT[:, half * 4:(half + 1) * 4, :].rearrange("p k t -> p (k t)"), in0=bank(b6),
                    in1=sg1T[ub][:, half * 4:(half + 1) * 4, :].rearrange("p k t -> p (k t)"), op=ALU.mult),
                    reads=["sg1T%d" % ub] + bk(b6), writes=["xnT"], excl=bk(b6))
            yield 1
            for half in range(2):
                b6, b7 = side_pair[0]
                for fc in range(8):
                    mm(bank(b7), xnT[:, fc, :], Wo1[:, fc, half * 512:(half + 1) * 512], fc == 0, fc == 7, ["xnT", "Wo1"], [b7])
                yield 1
                P.op("dve", lambda e, half=half, b6=b6, b7=b7, tp=tp: e.tensor_tensor(out=xt[xi][:, half * 512:(half + 1) * 512],
                                                                 in0=xt[xi][:, half * 512:(half + 1) * 512], in1=bank(b7), op=ALU.add),
                     reads=[xk] + bk(b7), writes=[xk], excl=bk(b7))
            P.dma(lambda e, b6=b6, b7=b7, tp=tp: e.dma_start(out=pb["ydst"], in_=xt[xi]), reads=[xk], chan="y%d" % xi, final=True)

        def q_phase(s, kbase):
            blocks = []
            if s == 0:
                for i in range(NT):
                    blocks.append(dict(slot=0, halo=False, first=(i == 0), x=xs_d[0][i * 128:(i + 1) * 128, :],
                                       rope=rope_d[i * 128:(i + 1) * 128, :], ydst=y_d[0][i * 128:(i + 1) * 128, :], ub=i % 2,
                                       corr=(0 if i == 0 else (1 if i == NT - 1 else None))))
            else:
                blocks.append(dict(slot=1, halo=True, first=False, x=xh_d[:, :], rope=ropeh_d[:, :], ydst=None, ub=0, corr=None))
                for i in range(NB1):
                    blocks.append(dict(slot=1, halo=False, first=(i == 0), x=xq1_d[i * 128:(i + 1) * 128, :],
                                       rope=ropeq1_d[i * 128:(i + 1) * 128, :], ydst=y_d[1][i * 128:(i + 1) * 128, :], ub=i % 2,
                                       corr=(2 if i == 0 else (3 if i == NB1 - 1 else None))))
            for n, b in enumerate(blocks):
                b["k"] = kbase + n
            nb = len(blocks)

            def load_front(b):
                xi = (b["k"] - 1) % 3
                P.dma(lambda e: e.dma_start(out=xt[xi], in_=b["x"]), writes=["xt%d" % xi], chan="xt%d" % xi)

            def load_back(b):
                xi = b["k"] % 3
                P.dma(lambda e: e.dma_start(out=xt[xi], in_=b["x"]), writes=["xt%d" % xi], chan="xt%d" % xi)

            load_front(blocks[0])
            drain([[front_gen(blocks[0]), 0]])
            for n, b in enumerate(blocks):
                def side():
                    if n >= 1:
                        pv_ = blocks[n - 1]
                        pp = blocks[n - 2] if n >= 2 else None
                        yield from back_gen(pv_, pp)
                    if n + 1 < nb:
                        yield from front_gen(blocks[n + 1])
                if n >= 1:
                    load_back(blocks[n - 1])
                if n + 1 < nb:
                    load_front(blocks[n + 1])
                gens = [[side(), 0]]
                attention(b, gens)
                nleft = 0
                while gens:
                    step_all(gens)
                    nleft += 1
                if n == 1:
                    print("[build] slot %d: side steps left after attention: %d" % (s, nleft))
            last = blocks[-1]
            load_back(last)
            drain([[back_gen(last, blocks[-2] if nb >= 2 else None), 0]])
            ub = last["ub"]
            uk = "UT%d" % ub
            if s == 0:
                P.op("pool", lambda e: e.memset(UT[ub][:, :, 136:144], 0.0), writes=[uk])
            else:
                P.op("pool", lambda e: e.tensor_copy(out=UT[ub][:, :, 136:144], in_=UH[:, :, 8:16]), reads=["UH"], writes=[uk])
            drain([[l1back_gen(last), 0]])
            return nb

        kbase = 0
        mod_pass()
        for s in range(2):
            slot_prep(s)
            kv_phase(s)
            P.fence()
            kbase += q_phase(s, kbase)

        sems = {e: es.enter_context(nc.semaphore("sem_" + e)) for e in ENGS}
        dsems = {c: es.enter_context(nc.semaphore("dsem_" + c)) for c in P.channels()}
        with nc.Block() as block:
            P.emit(block, sems, dsems)
    return nc


def host_prep(inp, S):
    f32 = np.float32
    QW = S // 4
    xsamp = np.asarray(inp["x_sample"], f32)
    xprom = np.asarray(inp["x_prompt"], f32)
    csamp = np.asarray(inp["c_sample"], f32)
    cprom = np.asarray(inp["c_prompt"], f32)
    norm_g = np.asarray(inp["norm_g"], f32)
    ada_w = np.ascontiguousarray(np.asarray(inp["ada_w"], f32))
    ada_b = np.ascontiguousarray(np.asarray(inp["ada_b"], f32))
    pos = np.arange(S)
    rowp = (pos // GRID_W).astype(f32)
    colp = (pos % GRID_W).astype(f32)
    inv = (f32(10000.0) ** (-(np.arange(0, HD // 2, 2, dtype=f32)) / f32(HD // 2))).astype(f32)
    ang = np.concatenate([rowp[:, None] * inv[None, :], colp[:, None] * inv[None, :]], axis=-1).astype(f32)
    rope = np.concatenate([np.cos(ang), np.sin(ang)], axis=-1).astype(f32)

    def corr_for(tpos):
        out = np.ones((4, 128), f32)
        for j, w in enumerate(WINS):
            lo = np.clip(tpos - w // 2, 0, S)
            hi = np.clip(tpos - w // 2 + w, 0, S)
            out[j] = w / (hi - lo).astype(f32)
        return out

    common = {
        "rope": rope,
        "gcol": np.ascontiguousarray(norm_g.reshape(2, 8, 128).transpose(2, 0, 1).reshape(128, 16)),
        "abcol": np.ascontiguousarray(np.repeat(ada_b[:, 0:2048].reshape(2, 16, 128).transpose(2, 0, 1).reshape(128, 32), 2, axis=1)),
        "ada_b": ada_b,
        "ada_w": ada_w,
        "w_in0": np.ascontiguousarray(np.asarray(inp["attn_w_in"], f32)[0]),
        "w_out0": np.ascontiguousarray(np.asarray(inp["attn_w_out"], f32)[0]),
        "w_in1": np.ascontiguousarray(np.asarray(inp["pool_w_in"], f32)[0]),
        "w_grp": np.ascontiguousarray(np.asarray(inp["pool_w_group"], f32)[0].reshape(1024, 256)),
        "w_out1": np.ascontiguousarray(np.asarray(inp["pool_w_out"], f32)[0]),
        "qg": np.ascontiguousarray(np.asarray(inp["attn_q_norm"], f32)[0:1]),
        "kg": np.ascontiguousarray(np.asarray(inp["attn_k_norm"], f32)[0:1]),
        "pscale": np.ascontiguousarray(np.asarray(inp["pool_scale"], f32)[0:1]),
    }
    in_maps = []
    for c in range(8):
        pb, r = c // 4, c % 4
        q0 = r * QW
        xs1 = xprom[pb]
        xh = np.zeros((128, D), f32)
        hpos = np.zeros(128, np.int64)
        maskh = np.zeros((1, 16), f32)
        if r > 0:
            xh[0:8] = xs1[q0 - 8:q0]
            hpos[0:8] = np.arange(q0 - 8, q0)
            maskh[0, 0:8] = 1.0
        if r < 3:
            xh[8:16] = xs1[q0 + QW:q0 + QW + 8]
            hpos[8:16] = np.arange(q0 + QW, q0 + QW + 8)
            maskh[0, 8:16] = 1.0
        cvec = np.stack([csamp[c], cprom[pb]], 0)
        cT = np.ascontiguousarray(cvec.reshape(2, 8, 128).transpose(2, 1, 0).reshape(128, 16))
        corr = np.stack([corr_for(np.arange(0, 128)), corr_for(np.arange(S - 128, S)),
                         corr_for(np.arange(q0, q0 + 128)), corr_for(np.arange(q0 + QW - 128, q0 + QW))], 0)
        m = dict(common)
        m.update({
            "xs0": np.ascontiguousarray(xsamp[c]),
            "xs1": np.ascontiguousarray(xs1),
            "xq1": np.ascontiguousarray(xs1[q0:q0 + QW]),
            "xh": xh,
            "cT": cT,
            "ropeq1": np.ascontiguousarray(rope[q0:q0 + QW]),
            "ropeh": np.ascontiguousarray(rope[hpos]),
            "corr": np.ascontiguousarray(corr.reshape(1, -1)),
            "maskh": maskh,
        })
        in_maps.append(m)
    return in_maps


_NC_CACHE = {}


def run(inputs, S):
    if S not in _NC_CACHE:
        _NC_CACHE[S] = build_program(S)
    nc = _NC_CACHE[S]
    in_maps = host_prep(inputs, S)
    res = run_bass_kernel_spmd(nc, in_maps, core_ids=list(range(8)))
    QW = S // 4
    y_sample = np.stack([np.asarray(res.results[c]["y0"], np.float32) for c in range(8)], 0)
    y_prompt = np.zeros((2, S, D), np.float32)
    for c in range(8):
        y_prompt[c // 4, (c % 4) * QW:(c % 4 + 1) * QW] = np.asarray(res.results[c]["y1"], np.float32)
    return y_prompt, y_sample


def kernel(**inputs):
    S = int(np.asarray(inputs["x_sample"]).shape[1])
    return run(inputs, S)
```
